# Optimizing a Trainium2 kernel written in Bass

```python
import functools
import jax, jax.numpy as jnp
from jax import lax
import numpy as np

D_MODEL = 1024
BATCH = 16
SEQ = 256
DEPTH = 1
DEC_BATCH = 8
DEC_SEQ = 2048
PAST_LEN = 256

GRID_W = 64
HEAD_DIM = 64
NA_HEADS = 8
NA_ROWS = 8
NA_COLS = 16
SWA_HEADS = 8
SWA_KV_HEADS = 2
SWA_GROUPS = SWA_HEADS // SWA_KV_HEADS
SWA_WINDOW = 128
BLOCK = 128
FFN_DIM = 2816
ROPE_BASE = 10000.0
EPS = 1e-6
NEG_INF = -1e30
N_MOD = 9
NA_WIDTH = NA_HEADS * HEAD_DIM
SWA_Q_WIDTH = SWA_HEADS * HEAD_DIM
SWA_KV_WIDTH = SWA_KV_HEADS * HEAD_DIM
IN_SPLITS = (NA_WIDTH, 2 * NA_WIDTH, 3 * NA_WIDTH,
             3 * NA_WIDTH + SWA_Q_WIDTH,
             3 * NA_WIDTH + SWA_Q_WIDTH + SWA_KV_WIDTH,
             3 * NA_WIDTH + SWA_Q_WIDTH + 2 * SWA_KV_WIDTH)
IN_COLS = IN_SPLITS[-1] + 2 * D_MODEL

kernel_name = "hybrid_diffusion_na_swa_step"


def rmsnorm(x, g):
    x32 = x.astype(jnp.float32)
    y = x32 * lax.rsqrt(jnp.mean(x32 * x32, axis=-1, keepdims=True) + EPS)
    return y.astype(x.dtype) * g


def adaln(cond, w_ada, b_ada):
    return (jax.nn.silu(cond) @ w_ada + b_ada).reshape(cond.shape[0], N_MOD, D_MODEL)


def modulate(x, g, shift, scale):
    return rmsnorm(x, g) * (1.0 + scale) + shift


def swiglu(h, w_gate, w_up, w_down):
    return (jax.nn.silu(h @ w_gate) * (h @ w_up)) @ w_down


def rope_1d(x, pos):
    half = x.shape[-1] // 2
    freqs = jnp.power(ROPE_BASE, -jnp.arange(half, dtype=jnp.float32) / half)
    ang = pos.astype(jnp.float32)[:, None] * freqs[None, :]
    cos = jnp.cos(ang)[None, :, None, :].astype(x.dtype)
    sin = jnp.sin(ang)[None, :, None, :].astype(x.dtype)
    x1, x2 = x[..., :half], x[..., half:]
    return jnp.concatenate([x1 * cos - x2 * sin, x2 * cos + x1 * sin], axis=-1)


def axial_rope(x):
    t = jnp.arange(x.shape[1])
    return jnp.concatenate([rope_1d(x[..., :HEAD_DIM // 2], t // GRID_W),
                            rope_1d(x[..., HEAD_DIM // 2:], t % GRID_W)], axis=-1)


def softmax_f32(logits):
    return jax.nn.softmax(logits.astype(jnp.float32), axis=-1)


def project_in(h, w_in):
    b, n = h.shape[:2]
    qa, ka, va, qb, kb, vb, gates = jnp.split(h @ w_in, IN_SPLITS, axis=-1)
    heads = lambda t, nh: t.reshape(b, n, nh, HEAD_DIM)
    return (heads(qa, NA_HEADS), heads(ka, NA_HEADS), heads(va, NA_HEADS),
            heads(qb, SWA_HEADS), heads(kb, SWA_KV_HEADS), heads(vb, SWA_KV_HEADS), gates)


def merge_branches(y_na, y_swa, gates, w_branch_na, w_branch_swa, w_out):
    g_na, g_swa = jnp.split(gates, 2, axis=-1)
    merged = jax.nn.sigmoid(g_na) * (y_na @ w_branch_na) + jax.nn.sigmoid(g_swa) * (y_swa @ w_branch_swa)
    return merged @ w_out


def ctx_attention(q, k, v, sink):
    b, l, kvh, g, d = q.shape
    scale = d ** -0.5
    q_blocks = q.reshape(b, l // BLOCK, BLOCK, kvh, g, d).transpose(1, 0, 2, 3, 4, 5)

    def one_block(q_blk):
        logits = jnp.einsum('bqkgd,blkd->bkgql', q_blk, k).astype(jnp.float32) * scale
        if sink is not None:
            s = jnp.broadcast_to(sink.reshape(kvh, g, 1, 1).astype(jnp.float32), (b, kvh, g, BLOCK, 1))
            logits = jnp.concatenate([logits, s], axis=-1)
        p = softmax_f32(logits)[..., :l].astype(v.dtype)
        return jnp.einsum('bkgql,blkd->bqkgd', p, v)

    out = lax.map(one_block, q_blocks)
    return out.transpose(1, 0, 2, 3, 4, 5).reshape(b, l, kvh * g * d)


def na_latent(q, k, v, ck, cv, rel_bias):
    b, n, h, d = q.shape
    rows = n // GRID_W
    kr = min(NA_ROWS, rows)
    l = ck.shape[1]
    scale = d ** -0.5
    qg = q.reshape(b, rows, GRID_W, h, d)
    kg = k.reshape(b, rows, GRID_W, h, d)
    vg = v.reshape(b, rows, GRID_W, h, d)
    r = np.arange(rows)
    row_start = np.clip(r - kr // 2, 0, rows - kr)
    row_idx = row_start[:, None] + np.arange(kr)[None, :]
    k_blk = kg[:, row_idx]
    v_blk = vg[:, row_idx]
    cq = np.arange(GRID_W)
    col_start = np.clip(cq - NA_COLS // 2, 0, GRID_W - NA_COLS)
    kc = np.arange(GRID_W)
    col_mask = (kc[None, :] >= col_start[:, None]) & (kc[None, :] < col_start[:, None] + NA_COLS)
    dr = row_idx - r[:, None] + (NA_ROWS - 1)
    dc = np.clip(kc[None, :] - cq[:, None], -(NA_COLS - 1), NA_COLS - 1) + (NA_COLS - 1)
    bias = rel_bias[:, dr[:, None, :, None], dc[None, :, None, :]].astype(jnp.float32)
    nb_logits = jnp.einsum('brqhd,brikhd->bhrqik', qg, k_blk).astype(jnp.float32) * scale + bias[None]
    nb_logits = jnp.where(col_mask[None, None, None, :, None, :], nb_logits, NEG_INF)
    nb_logits = nb_logits.reshape(b, h, rows, GRID_W, kr * GRID_W)
    ctx_logits = jnp.einsum('brqhd,blhd->bhrql', qg, ck).astype(jnp.float32) * scale
    p = softmax_f32(jnp.concatenate([nb_logits, ctx_logits], axis=-1))
    p_nb = p[..., :kr * GRID_W].reshape(b, h, rows, GRID_W, kr, GRID_W).astype(v.dtype)
    p_ctx = p[..., kr * GRID_W:].astype(v.dtype)
    out = (jnp.einsum('bhrqik,brikhd->brqhd', p_nb, v_blk)
           + jnp.einsum('bhrql,blhd->brqhd', p_ctx, cv))
    return out.reshape(b, n, h * d)


def swa_latent(q, k, v, ck, cv, sink):
    b, n = q.shape[:2]
    nb = n // BLOCK
    l = ck.shape[1]
    scale = HEAD_DIM ** -0.5
    qb = q.reshape(b, nb, BLOCK, SWA_KV_HEADS, SWA_GROUPS, HEAD_DIM)

    def band(t):
        tp = jnp.pad(t, ((0, 0), (BLOCK, BLOCK), (0, 0), (0, 0))).reshape(b, nb + 2, BLOCK, SWA_KV_HEADS, HEAD_DIM)
        return jnp.concatenate([tp[:, :-2], tp[:, 1:-1], tp[:, 2:]], axis=2)

    k_band, v_band = band(k), band(v)
    qi = np.arange(BLOCK)[:, None]
    kj = np.arange(3 * BLOCK)[None, :]
    key_pos = (np.arange(nb)[:, None, None] - 1) * BLOCK + kj[None]
    mask = (np.abs(kj - BLOCK - qi)[None] <= SWA_WINDOW) & (key_pos >= 0) & (key_pos < n)
    band_logits = jnp.einsum('bnqkgd,bnjkd->bkgnqj', qb, k_band).astype(jnp.float32) * scale
    band_logits = jnp.where(mask, band_logits, NEG_INF)
    ctx_logits = jnp.einsum('bnqkgd,blkd->bkgnql', qb, ck).astype(jnp.float32) * scale
    sink_col = jnp.broadcast_to(sink.reshape(SWA_KV_HEADS, SWA_GROUPS, 1, 1, 1).astype(jnp.float32),
                                (b, SWA_KV_HEADS, SWA_GROUPS, nb, BLOCK, 1))
    p = softmax_f32(jnp.concatenate([band_logits, ctx_logits, sink_col], axis=-1))
    p_band = p[..., :3 * BLOCK].astype(v.dtype)
    p_ctx = p[..., 3 * BLOCK:3 * BLOCK + l].astype(v.dtype)
    out = (jnp.einsum('bkgnqj,bnjkd->bnqkgd', p_band, v_band)
           + jnp.einsum('bkgnql,blkd->bnqkgd', p_ctx, cv))
    return out.reshape(b, n, SWA_HEADS * HEAD_DIM)


def context_mixer(h, w_in, swa_sink, w_branch_na, w_branch_swa, w_out):
    b, l = h.shape[:2]
    qa, ka, va, qb, kb, vb, gates = project_in(h, w_in)
    y_na = ctx_attention(qa[:, :, :, None, :], ka, va, None)
    y_swa = ctx_attention(qb.reshape(b, l, SWA_KV_HEADS, SWA_GROUPS, HEAD_DIM), kb, vb, swa_sink)
    return merge_branches(y_na, y_swa, gates, w_branch_na, w_branch_swa, w_out), (ka, va, kb, vb)


def latent_mixer(h, ck_na, cv_na, ck_swa, cv_swa, w_in, na_rel_bias, swa_sink, w_branch_na, w_branch_swa, w_out):
    qa, ka, va, qb, kb, vb, gates = project_in(h, w_in)
    y_na = na_latent(qa, ka, va, ck_na, cv_na, na_rel_bias)
    y_swa = swa_latent(axial_rope(qb), axial_rope(kb), vb, ck_swa, cv_swa, swa_sink)
    return merge_branches(y_na, y_swa, gates, w_branch_na, w_branch_swa, w_out), ()


def macaron_layer(x, mod, mixer, norm_ffn1, f1_gate, f1_up, f1_down, norm_mix, norm_ffn2, f2_gate, f2_up, f2_down):
    sh1, sc1, gt1, shm, scm, gtm, sh2, sc2, gt2 = (mod[:, i, None, :] for i in range(N_MOD))
    x = x + 0.5 * gt1 * swiglu(modulate(x, norm_ffn1, sh1, sc1), f1_gate, f1_up, f1_down)
    y, aux = mixer(modulate(x, norm_mix, shm, scm))
    x = x + gtm * y
    x = x + 0.5 * gt2 * swiglu(modulate(x, norm_ffn2, sh2, sc2), f2_gate, f2_up, f2_down)
    return x, aux


def setup_inputs(seed: int = 0) -> dict:
    key = jax.random.key(seed)
    ks = jax.random.split(key, 32)
    nrm = lambda k, shape, s: jax.random.normal(k, shape, jnp.float32) * s
    D, F = D_MODEL, FFN_DIM
    return {
        "x_prompt": nrm(ks[0], (BATCH, SEQ, D), 1.0),
        "x_sample": nrm(ks[1], (DEC_BATCH, DEC_SEQ, D), 1.0),
        "cache_na_k": nrm(ks[2], (DEC_BATCH, DEPTH, PAST_LEN, NA_HEADS, HEAD_DIM), 1.0),
        "cache_na_v": nrm(ks[3], (DEC_BATCH, DEPTH, PAST_LEN, NA_HEADS, HEAD_DIM), 1.0),
        "cache_swa_k": nrm(ks[4], (DEC_BATCH, DEPTH, PAST_LEN, SWA_KV_HEADS, HEAD_DIM), 1.0),
        "cache_swa_v": nrm(ks[5], (DEC_BATCH, DEPTH, PAST_LEN, SWA_KV_HEADS, HEAD_DIM), 1.0),
        "c": nrm(ks[6], (DEC_BATCH, D), 1.0),
        "c_ctx": nrm(ks[7], (D,), 1.0),
        "w_ada": nrm(ks[8], (DEPTH, D, N_MOD * D), 0.5 * D ** -0.5),
        "b_ada": nrm(ks[9], (DEPTH, N_MOD * D), 0.01),
        "norm_ffn1": 1.0 + nrm(ks[10], (DEPTH, D), 0.01),
        "ffn1_w_gate": nrm(ks[11], (DEPTH, D, F), D ** -0.5),
        "ffn1_w_up": nrm(ks[12], (DEPTH, D, F), D ** -0.5),
        "ffn1_w_down": nrm(ks[13], (DEPTH, F, D), F ** -0.5),
        "norm_mix": 1.0 + nrm(ks[14], (DEPTH, D), 0.01),
        "w_in": nrm(ks[15], (DEPTH, D, IN_COLS), D ** -0.5),
        "na_rel_bias": nrm(ks[16], (DEPTH, NA_HEADS, 2 * NA_ROWS - 1, 2 * NA_COLS - 1), 0.1),
        "swa_sink": nrm(ks[17], (DEPTH, SWA_HEADS), 0.5),
        "w_branch_na": nrm(ks[18], (DEPTH, NA_WIDTH, D), NA_WIDTH ** -0.5),
        "w_branch_swa": nrm(ks[19], (DEPTH, SWA_Q_WIDTH, D), SWA_Q_WIDTH ** -0.5),
        "w_out": nrm(ks[20], (DEPTH, D, D), D ** -0.5),
        "norm_ffn2": 1.0 + nrm(ks[21], (DEPTH, D), 0.01),
        "ffn2_w_gate": nrm(ks[22], (DEPTH, D, F), D ** -0.5),
        "ffn2_w_up": nrm(ks[23], (DEPTH, D, F), D ** -0.5),
        "ffn2_w_down": nrm(ks[24], (DEPTH, F, D), F ** -0.5),
        "norm_final": 1.0 + nrm(ks[25], (D,), 0.01),
    }


def reference(x_prompt, x_sample, cache_na_k, cache_na_v, cache_swa_k, cache_swa_v, c, c_ctx,
              w_ada, b_ada, norm_ffn1, ffn1_w_gate, ffn1_w_up, ffn1_w_down, norm_mix, w_in,
              na_rel_bias, swa_sink, w_branch_na, w_branch_swa, w_out, norm_ffn2,
              ffn2_w_gate, ffn2_w_up, ffn2_w_down, norm_final):
    x = x_prompt
    new_na_k, new_na_v, new_swa_k, new_swa_v = [], [], [], []
    for layer in range(DEPTH):
        mod = adaln(c_ctx[None, :], w_ada[layer], b_ada[layer])
        mixer = functools.partial(context_mixer, w_in=w_in[layer], swa_sink=swa_sink[layer],
                                  w_branch_na=w_branch_na[layer], w_branch_swa=w_branch_swa[layer],
                                  w_out=w_out[layer])
        x, (ka, va, kb, vb) = macaron_layer(x, mod, mixer, norm_ffn1[layer], ffn1_w_gate[layer],
                                            ffn1_w_up[layer], ffn1_w_down[layer], norm_mix[layer],
                                            norm_ffn2[layer], ffn2_w_gate[layer], ffn2_w_up[layer],
                                            ffn2_w_down[layer])
        new_na_k.append(ka)
        new_na_v.append(va)
        new_swa_k.append(kb)
        new_swa_v.append(vb)
    y_prompt = rmsnorm(x, norm_final)

    x = x_sample
    for layer in range(DEPTH):
        mod = adaln(c, w_ada[layer], b_ada[layer])
        mixer = functools.partial(latent_mixer, ck_na=cache_na_k[:, layer], cv_na=cache_na_v[:, layer],
                                  ck_swa=cache_swa_k[:, layer], cv_swa=cache_swa_v[:, layer],
                                  w_in=w_in[layer], na_rel_bias=na_rel_bias[layer], swa_sink=swa_sink[layer],
                                  w_branch_na=w_branch_na[layer], w_branch_swa=w_branch_swa[layer],
                                  w_out=w_out[layer])
        x, _ = macaron_layer(x, mod, mixer, norm_ffn1[layer], ffn1_w_gate[layer],
                             ffn1_w_up[layer], ffn1_w_down[layer], norm_mix[layer],
                             norm_ffn2[layer], ffn2_w_gate[layer], ffn2_w_up[layer],
                             ffn2_w_down[layer])
    y_sample = rmsnorm(x, norm_final)

    return (y_prompt, y_sample, jnp.stack(new_na_k, axis=1), jnp.stack(new_na_v, axis=1),
            jnp.stack(new_swa_k, axis=1), jnp.stack(new_swa_v, axis=1))
```

```python
import numpy as np
from contextlib import ExitStack
import concourse.bass as bass
import concourse.mybir as mybir
from concourse.bass_utils import run_bass_kernel_spmd

F32 = mybir.dt.float32
BF16 = mybir.dt.bfloat16
AF = mybir.ActivationFunctionType
ALU = mybir.AluOpType

D = 1024
FF = 2816
NCH = 8
NF = 22
T = 2560
TS = 2048
TILE = 512
NT = 5
EPS = 1e-6
NEG = -30000.0
ENG_NAMES = ("pe", "act", "dve", "pool", "sp")


class _Stop(Exception):
    pass


class Sched:
    N_DMA_SEMS = 8
    limit = 0
    nrec = 0

    def _count(self):
        self.nrec += 1
        if self.limit and self.nrec >= self.limit:
            raise _Stop()

    def __init__(self, nc, stack):
        self.nc = nc
        self.ops = {e: [] for e in ENG_NAMES}
        self.known = {e: {} for e in ENG_NAMES}
        self.last_w = {}
        self.readers = {}
        self.sem = {e: stack.enter_context(nc.semaphore("prog_" + e)) for e in ("pe", "act", "dve", "pool")}
        self.dma_sems, self.dma_next, self.dma_target = {}, {}, {}
        for q in ("sp", "pool", "act"):
            self.dma_sems[q] = [stack.enter_context(nc.semaphore(f"dma_{q}_{i}")) for i in range(self.N_DMA_SEMS)]
            self.dma_next[q] = 0
            self.dma_target[q] = [0] * self.N_DMA_SEMS
        self.final_tokens = []

    def _need(self, eng, tok, waits):
        if tok is None:
            return
        if tok[0] == "eng":
            _, f, idx = tok
            if f == eng and eng == "pe":
                return
            key = ("eng", f)
            if self.known[eng].get(key, -1) >= idx:
                return
            self.known[eng][key] = idx
            waits.append(tok)
            self.ops[f][idx]["needed"] = True
        else:
            _, q, si, target = tok
            key = ("dma", q, si)
            if self.known[eng].get(key, -1) >= target:
                return
            self.known[eng][key] = target
            waits.append(tok)

    def _deps(self, eng, reads, writes):
        waits = []
        for r in reads:
            self._need(eng, self.last_w.get(r), waits)
        for w in writes:
            self._need(eng, self.last_w.get(w), waits)
            for t in self.readers.get(w, ()):
                self._need(eng, t, waits)
        return waits

    def _commit(self, tok, reads, writes):
        for r in reads:
            self.readers.setdefault(r, []).append(tok)
        for w in writes:
            self.last_w[w] = tok
            self.readers[w] = []

    def op(self, eng, fn, reads=(), writes=()):
        self._count()
        psr = [r for r in reads if isinstance(r, tuple) and r[0] == "ps"]
        if psr:
            reads = [r for r in reads if r not in psr]
            writes = list(writes) + [r for r in psr if r not in writes]
        waits = self._deps(eng, reads, writes)
        idx = len(self.ops[eng])
        self.ops[eng].append(dict(kind="op", fn=fn, waits=waits, needed=False))
        tok = ("eng", eng, idx)
        self._commit(tok, reads, writes)
        return tok

    def dma(self, q, out, in_, reads=(), writes=(), final=False, **kw):
        self._count()
        waits = self._deps(q, reads, writes)
        si = self.dma_next[q]
        self.dma_next[q] = (si + 1) % self.N_DMA_SEMS
        prev_target = self.dma_target[q][si]
        if prev_target > 0:
            self._need(q, ("dma", q, si, prev_target), waits)
        target = prev_target + 16
        self.dma_target[q][si] = target
        self.ops[q].append(dict(kind="dma", out=out, in_=in_, waits=waits, si=si, kw=kw, needed=False))
        tok = ("dma", q, si, target)
        self._commit(tok, reads, writes)
        if final:
            self.final_tokens.append(tok)
        return tok

    def barrier(self):
        toks = []
        for e in ("pe", "act", "dve", "pool"):
            for i in range(len(self.ops[e]) - 1, -1, -1):
                if self.ops[e][i]["kind"] == "op":
                    toks.append(("eng", e, i))
                    break
        for q in ("sp", "pool", "act"):
            for si in range(self.N_DMA_SEMS):
                if self.dma_target[q][si] > 0:
                    toks.append(("dma", q, si, self.dma_target[q][si]))
        for e in ENG_NAMES:
            waits = []
            for tk in toks:
                if tk[0] == "eng" and tk[1] == e:
                    continue
                self._need(e, tk, waits)
            if waits:
                self.ops[e].append(dict(kind="wait", waits=waits, needed=False))

    def emit(self):
        nc = self.nc
        fw = []
        for tok in self.final_tokens:
            self._need("sp", tok, fw)
        cum = {}
        for e in ("pe", "act", "dve", "pool"):
            c, arr = 0, []
            for o in self.ops[e]:
                if o["kind"] == "op" and o["needed"]:
                    c += 1
                arr.append(c)
            cum[e] = arr

        def emit_wait(engh, tok):
            if tok[0] == "eng":
                engh.wait_ge(self.sem[tok[1]], cum[tok[1]][tok[2]])
            else:
                engh.wait_ge(self.dma_sems[tok[1]][tok[2]], tok[3])

        def run(ename, engh):
            for o in self.ops[ename]:
                for tok in o["waits"]:
                    emit_wait(engh, tok)
                if o["kind"] == "op":
                    ins = o["fn"](engh)
                    if o["needed"]:
                        ins.then_inc(self.sem[ename], 1)
                elif o["kind"] == "dma":
                    engh.dma_start(out=o["out"], in_=o["in_"], **o["kw"]).then_inc(
                        self.dma_sems[ename][o["si"]], 16)
            if ename == "sp":
                for tok in fw:
                    emit_wait(engh, tok)

        with nc.Block() as block:
            @block.tensor
            def _(e):
                run("pe", e)

            @block.scalar
            def _(e):
                run("act", e)

            @block.vector
            def _(e):
                run("dve", e)

            @block.gpsimd
            def _(e):
                run("pool", e)

            @block.sync
            def _(e):
                run("sp", e)


def L(method, *a, **k):
    return lambda e: getattr(e, method)(*a, **k)


W_SPECS = [
    ("w_ada", [D, 9 * D]), ("b_ada", [9 * D]), ("norm_ffn1", [D]),
    ("f1g", [D, FF]), ("f1u", [D, FF]), ("f1d", [FF, D]),
    ("norm_mix", [D]), ("w_in", [D, 4352]), ("rel_bias", [120, 31]), ("sink", [8]),
    ("wb_na", [512, D]), ("wb_sw", [512, D]), ("w_out", [D, D]),
    ("norm_ffn2", [D]), ("f2g", [D, FF]), ("f2u", [D, FF]), ("f2d", [FF, D]), ("norm_final", [D]),
]


def build(stop=0):
    nc = bass.Bass("TRN2", target_bir_lowering=False)
    din = lambda name, shape: nc.dram_tensor(name, shape, F32, kind="ExternalInput").ap()
    dout = lambda name, shape: nc.dram_tensor(name, shape, F32, kind="ExternalOutput").ap()
    x_tok = din("x_tok", [T, D])
    cond = din("cond", [2, D])
    cna_k = din("cna_k", [256, 512]); cna_v = din("cna_v", [256, 512])
    csw_k = din("csw_k", [256, 128]); csw_v = din("csw_v", [256, 128])
    Wd = {n: din(n, s) for n, s in W_SPECS}
    c_ident = din("c_ident", [128, 128])
    c_jrev = din("c_jrev", [128, 128])
    c_cos = din("c_cos", [128, TS]); c_sin = din("c_sin", [128, TS])
    c_band = din("c_band", [128, 384])
    c_negm = din("c_negm", [128, 64])
    y_tok = dout("y_tok", [T, D])
    nk_na = dout("nk_na", [512, 512]); nv_na = dout("nv_na", [512, 512])
    nk_sw = dout("nk_sw", [512, 128]); nv_sw = dout("nv_sw", [512, 128])
    if stop:
        dbg = dout("dbg", [128, NCH, T])
        dbg2 = dout("dbg2", [128, 144])
    scrP_h = nc.dram_tensor("scrP", [120, 127], F32)
    scrX_h = nc.dram_tensor("scrX", [128, NCH, T], F32)
    scrX = scrX_h.ap()

    with ExitStack() as st:
        S = Sched(nc, st)
        if stop >= 1000:
            S.limit = stop

        def sb(name, shape, dt, stack=st):
            return stack.enter_context(nc.sbuf_tensor(name, shape, dt))

        open_scopes = []

        def new_scope():
            sc = ExitStack()
            open_scopes.append(sc)
            return sc

        def close_scope(sc):
            assert open_scopes[-1] is sc
            open_scopes.pop()
            sc.close()

        def chk(k):
            if stop == k:
                raise _Stop()

        try:
            banks = [st.enter_context(nc.psum_tensor(f"bank{i}", [128, 512], F32)) for i in range(8)]
            PK = lambda b: ("ps", b)
            rot = {}

            def nextbank(group, ids):
                i = rot.get(group, 0)
                rot[group] = i + 1
                return ids[i % len(ids)]

            hT = sb("hT", [128, NCH, T], BF16)
            wslot = [sb(f"wslot{i}", [128, 6144], BF16) for i in range(2)]
            ident_f = sb("ident_f", [128, 128], F32)
            ident_b = sb("ident_b", [128, 128], BF16)
            jrev_b = sb("jrev_b", [128, 128], BF16)
            ones_b = sb("ones_b", [128, 128], BF16)
            scal = sb("scal", [128, 3, 3, NCH, 2], F32)
            prm = sb("prm", [128, 120], F32)
            prow = sb("prow", [120, 128], F32)
            badaT = prm[:, 0:72]
            gains = prm[:, 72:96].rearrange("p (i c) -> p i c", i=3)
            gfin = prm[:, 96:104]
            condT = prm[:, 104:120].rearrange("p (k c) -> p c k", k=2)
            modT = sb("modT", [128, 72, 2], F32)
            scT = sb("scT", [128, NCH, 2], BF16)
            es = sb("es", [128, 8], F32)
            sqb = [sb(f"sqb{i}", [128, 512], BF16) for i in range(4)]
            rstd = [sb(f"rstd{i}", [128, 512], F32) for i in range(3)]
            tmpn = [sb(f"tmpn{i}", [128, 512], F32) for i in range(4)]
            zP = sb("zP", [120, 127], F32)

            wq = "pool"

            S.dma("sp", ident_f[:], c_ident, writes=["ident_f"])
            S.dma(wq, ident_b[:], c_ident, writes=["ident_b"])
            S.dma(wq, jrev_b[:], c_jrev, writes=["jrev_b"])
            S.op("dve", L("memset", ones_b[:], 1.0 / D), writes=["ones_b"])
            S.dma("sp", prow[0:72, :], Wd["b_ada"].rearrange("(j p) -> j p", p=128), writes=["prow"])
            for i, nm in enumerate(("norm_ffn1", "norm_mix", "norm_ffn2", "norm_final")):
                S.dma("sp", prow[72 + 8 * i:80 + 8 * i, :], Wd[nm].rearrange("(c p) -> c p", p=128), writes=["prow"])
            for k_ in range(2):
                S.dma("sp", prow[104 + 8 * k_:112 + 8 * k_, :], cond[k_, :].rearrange("(c p) -> c p", p=128), writes=["prow"])
            S.op("pe", L("transpose", banks[7][:, 0:120], prow[:, :], ident_f[0:120, 0:120]), reads=["prow", "ident_f"],
                 writes=[PK(7)])
            S.op("dve", L("tensor_copy", prm[:], banks[7][:, 0:120]), reads=[PK(7)], writes=["prm"])
            def late_consts():
                S.dma("sp", es[:], bass.AP(Wd["sink"].tensor, 0, [[0, 128], [1, 8]]), writes=["es"])
                S.op("dve", L("memset", zP[:], 0.0), writes=["zP"])
                S.dma("sp", zP[:, 48:79], Wd["rel_bias"], writes=["zP"], reads=["zP"])
                S.dma("sp", scrP_h.ap(), zP[:], reads=["zP"], writes=["scrP"])

            def load_x(xT, stg, tiles=range(NT), after=()):
                ns = len(stg)
                for tb in [4 * t_ + i_ for t_ in tiles for i_ in range(4)]:
                    xs = stg[tb % ns]
                    S.dma("sp", xs[:], x_tok[tb * 128:(tb + 1) * 128, :], reads=list(after), writes=[("xs", tb % ns)])
                    for half in range(2):
                        b = nextbank("trx", [0, 1, 2, 3, 4, 5])
                        for j in range(4):
                            c = half * 4 + j
                            S.op("pe", L("transpose", banks[b][:, j * 128:(j + 1) * 128], xs[:, c * 128:(c + 1) * 128], ident_f[:]),
                                reads=[("xs", tb % ns), "ident_f"], writes=[PK(b)])
                        dst = xT[:, half * 4:half * 4 + 4, tb * 128:(tb + 1) * 128]
                        src = banks[b][:].rearrange("p (j n) -> p j n", j=4)
                        if half == 0:
                            S.op("act", L("copy", dst, src), reads=[PK(b)],
                                 writes=[("xT", tb // 4)])
                        else:
                            S.op("dve", L("tensor_copy", dst, src), reads=[PK(b)],
                                 writes=[("xT", tb // 4)])

            ada_bank = 7

            def ada_buf(piece, adas):
                if piece in (2, 3):
                    return wslot[piece % 2][:, 0:4096], ("wslot", piece % 2)
                return adas[piece % 2][:], ("adas", piece % 2)

            def ada_dma(piece, adas):
                buf, key = ada_buf(piece, adas)
                wv = buf.rearrange("p (c n) -> p c n", c=8)
                S.dma(wq, wv, Wd["w_ada"][:, piece * 512:(piece + 1) * 512].rearrange("(c p) n -> p c n", p=128),
                      writes=[key, ("adapiece", piece)])

            def ada_compute(piece, adas):
                buf, key = ada_buf(piece, adas)
                slot = key
                wv = buf.rearrange("p (c n) -> p c n", c=8)
                b = nextbank("dn", [4, 5, 6, 7])
                pm = banks[b][:, 0:8].rearrange("p (j k) -> p j k", k=2)
                for jj in range(4):
                    for k in range(8):
                        S.op("pe", L("matmul", pm[:, jj, :], wv[:, k, jj * 128:(jj + 1) * 128], scT[:, k, :],
                            start=(k == 0), stop=(k == 7), skip_group_check=True),
                            reads=[slot, "scT"], writes=[PK(b)])
                j0 = piece * 4
                S.op("dve", L("tensor_tensor", modT[:, j0:j0 + 4, :], pm,
                              badaT[:, j0:j0 + 4].unsqueeze(2).to_broadcast([128, 4, 2]), ALU.add),
                     reads=[PK(b), "prm"], writes=["modT"])

            def ada_scal(ph, parts=("ab", "g")):
                sh = modT[:, (3 * ph) * 8:(3 * ph + 1) * 8, :]
                sc = modT[:, (3 * ph + 1) * 8:(3 * ph + 2) * 8, :]
                gt = modT[:, (3 * ph + 2) * 8:(3 * ph + 3) * 8, :]
                gb = gains[:, ph, :].unsqueeze(2).to_broadcast([128, 8, 2])
                if "ab" in parts:
                    S.op("dve", L("scalar_tensor_tensor", scal[:, ph, 0, :, :], sc, 1.0, gb, ALU.add, ALU.mult),
                        reads=["modT", "prm"], writes=[("scal", ph, 0)])
                    S.op("dve", L("tensor_copy", scal[:, ph, 1, :, :], sh),
                         reads=["modT"], writes=[("scal", ph, 1)])
                if "g" in parts:
                    S.op("dve", L("tensor_scalar", scal[:, ph, 2, :, :], gt, 0.5 if ph != 1 else 1.0, None, ALU.mult),
                        reads=["modT"], writes=[("scal", ph, 2)])

            def norm_stats(xsrc, xkey):
                b = 6 if (rot.get("nrm", 0) % 2 == 0) else 7
                rot["nrm"] = rot.get("nrm", 0) + 1
                for c in range(NCH):
                    q = nextbank("sq", [0, 1, 2, 3])
                    S.op("act", L("activation", sqb[q][:], xsrc(c), AF.Square),
                         reads=[xkey], writes=[("sqb", q)])
                    S.op("pe", L("matmul", banks[b][:], ones_b[:], sqb[q][:],
                                                                 start=(c == 0), stop=(c == 7)),
                         reads=[("sqb", q), "ones_b"], writes=[PK(b)])
                r = nextbank("rstd", [0, 1, 2])
                S.op("act", L("activation", rstd[r][:], banks[b][:], AF.Sqrt, bias=EPS, scale=1.0),
                     reads=[PK(b)], writes=[("rstd", r)])
                return r

            def norm_recip(r):
                S.op("dve", L("reciprocal", rstd[r][:], rstd[r][:]),
                     reads=[("rstd", r)], writes=[("rstd", r)])

            def norm_mod_tile(xsrc, xkey, ph, t, r=None):
                kind = 0 if t < 4 else 1
                if r is None:
                    r = norm_stats(xsrc, xkey)
                norm_recip(r)
                sl = slice(t * TILE, (t + 1) * TILE)
                for c in range(NCH):
                    q = nextbank("tmpn", [0, 1, 2, 3])
                    if True:
                        S.op("dve", L("scalar_tensor_tensor", tmpn[q][:], xsrc(c), scal[:, ph, 0, c, kind:kind + 1], rstd[r][:], ALU.mult, ALU.mult),
                            reads=[xkey, ("rstd", r), ("scal", ph, 0)], writes=[("tmpn", q)])
                        S.op("act", L("activation", hT[:, c, sl], tmpn[q][:], AF.Identity, bias=scal[:, ph, 1, c, kind:kind + 1], scale=1.0),
                            reads=[("tmpn", q), ("scal", ph, 1)], writes=[("hT", t)])
                    else:
                        S.op("pool", L("tensor_tensor", tmpn[q][:], xsrc(c), rstd[r][:], ALU.mult),
                            reads=[xkey, ("rstd", r)], writes=[("tmpn", q)])
                        S.op("act", L("activation", hT[:, c, sl], tmpn[q][:], AF.Identity, bias=scal[:, ph, 1, c, kind:kind + 1],
                                      scale=scal[:, ph, 0, c, kind:kind + 1]),
                            reads=[("tmpn", q), ("scal", ph, 1), ("scal", ph, 0)], writes=[("hT", t)])

            def xtile(xT, t):
                return lambda c: xT[:, c, t * TILE:(t + 1) * TILE]

            def issue_na_weights(c):
                slot = c % 2
                wv = wslot[slot][:, 0:3072].rearrange("p (k n) -> p k n", k=8)
                for i, base in enumerate((0, 512, 1024)):
                    S.dma(wq, wv[:, :, i * 128:(i + 1) * 128],
                          Wd["w_in"][:, base + c * 128:base + (c + 1) * 128].rearrange("(k p) n -> p k n", p=128),
                          writes=[("wslot", slot)])

            def ffn_issue(g, wg, wu, wdn):
                G = 2
                slot = g % 2
                f0 = g * G * 128
                wgv = wslot[slot][:, 0:2048].rearrange("p (c n) -> p c n", c=8)
                wuv = wslot[slot][:, 2048:4096].rearrange("p (c n) -> p c n", c=8)
                wdv = wslot[slot][:, 4096:6144].rearrange("p (j n) -> p j n", j=G)
                S.dma(wq, wgv, wg[:, f0:f0 + G * 128].rearrange("(c p) n -> p c n", p=128), writes=[("wslot", slot)])
                S.dma(wq, wuv, wu[:, f0:f0 + G * 128].rearrange("(c p) n -> p c n", p=128), writes=[("wslot", slot)])
                S.dma(wq, wdv, wdn[f0:f0 + G * 128, :].rearrange("(j p) n -> p j n", p=128), writes=[("wslot", slot)])

            def ffn(xT, ph, wg, wu, wdn, actb, sgb, hook_dma=None, hook_pe=None, hook_final=None, dnb=(4, 5, 6), pre=0):
                G = 2
                ngrp = NF // G
                pend = None
                hstate = []

                def emit_down(g, t, slot, ab, dcs):
                    kind = 0 if t < 4 else 1
                    wdv = wslot[slot][:, 4096:6144].rearrange("p (j n) -> p j n", j=G)
                    sl = slice(t * TILE, (t + 1) * TILE)
                    for dc in dcs:
                        b = nextbank("dn", list(dnb))
                        for j in range(G):
                            S.op("pe", L("matmul", banks[b][:], wdv[:, j, dc * 128:(dc + 1) * 128], actb[ab][:, j, :],
                                start=(j == 0), stop=(j == G - 1)),
                                reads=[("wslot", slot), ("actb", ab)], writes=[PK(b)])
                        S.op("dve", L("scalar_tensor_tensor", xT[:, dc, sl], banks[b][:], scal[:, ph, 2, dc, kind:kind + 1], xT[:, dc, sl], ALU.mult, ALU.add),
                            reads=[PK(b), ("xT", t), ("scal", ph, 2)], writes=[("xT", t)])

                for g in range(ngrp):
                    slot = g % 2
                    f0 = g * G * 128
                    wgv = wslot[slot][:, 0:2048].rearrange("p (c n) -> p c n", c=8)
                    wuv = wslot[slot][:, 2048:4096].rearrange("p (c n) -> p c n", c=8)
                    wdv = wslot[slot][:, 4096:6144].rearrange("p (j n) -> p j n", j=G)
                    if g >= pre:
                        ffn_issue(g, wg, wu, wdn)
                    if hook_dma is not None:
                        hook_dma(g)
                    for t in range(NT):
                        if hook_pe is not None:
                            hook_pe(g, t)
                        ab = nextbank("actb", [0, 1, 2])
                        sl = slice(t * TILE, (t + 1) * TILE)
                        for j in range(G):
                            bg = nextbank("gu", [0, 1, 2, 3])
                            bu = nextbank("gu", [0, 1, 2, 3])
                            for (bb, wv) in ((bg, wgv), (bu, wuv)):
                                for c in range(NCH):
                                    S.op("pe", L("matmul", banks[bb][:], wv[:, c, j * 128:(j + 1) * 128], hT[:, c, sl],
                                        start=(c == 0), stop=(c == 7)),
                                        reads=[("wslot", slot), ("hT", t)], writes=[PK(bb)])
                            sq_ = nextbank("sgb", [0, 1])
                            S.op("act", L("activation", sgb[sq_][:], banks[bg][:], AF.Silu),
                                 reads=[PK(bg)], writes=[("sgb", sq_)])
                            S.op("dve", L("tensor_tensor", actb[ab][:, j, :], banks[bu][:], sgb[sq_][:], ALU.mult),
                                reads=[PK(bu), ("sgb", sq_)], writes=[("actb", ab)])
                            if pend is not None:
                                emit_down(*pend, range(j * 4, j * 4 + 4))
                                if j == G - 1 and hook_final is not None and pend[0] == ngrp - 1:
                                    if hstate:
                                        hook_final[1](*hstate.pop())
                                    hstate.append((pend[1], hook_final[0](pend[1])))
                        pend = (g, t, slot, ab)
                emit_down(*pend, range(8))
                if hook_final is not None:
                    if hstate:
                        hook_final[1](*hstate.pop())
                    tl = pend[1]
                    hook_final[1](tl, hook_final[0](tl))

            ffn_scope = new_scope()
            xT = sb("xT", [128, NCH, T], F32, ffn_scope)
            actb = [sb(f"actb{i}", [128, 2, 512], BF16, ffn_scope) for i in range(3)]
            sgb = [sb(f"sgb{i}", [128, 512], F32, ffn_scope) for i in range(2)]
            stg = [sb(f"stg{i}", [128, D], F32, ffn_scope) for i in range(3)]
            def stop_here(xsrc, scope):
                if xsrc is not None:
                    S.dma("sp", dbg, xsrc[:], reads=[("xT", t_) for t_ in range(NT)], final=True)
                S.dma("sp", dbg2, modT[:].rearrange("p j k -> p (j k)"), reads=["modT"], final=True)
                S.emit()
                close_scope(scope)
                return nc

            adas = [sb(f"adas{i}", [128, 4096], BF16, ffn_scope) for i in range(2)]
            S.op("act", L("activation", scT[:], condT[:], AF.Silu), reads=["prm"], writes=["scT"])
            for piece in range(4):
                ada_dma(piece, adas)
            load_x(xT, stg, [0])
            late_consts()
            if stop == 1:
                load_x(xT, stg, [1, 2, 3, 4])
                S.op("dve", L("memset", modT[:], 0.0), writes=["modT"])
                return stop_here(xT, ffn_scope)
            for piece in range(4):
                ada_compute(piece, adas)
                if piece == 2:
                    ffn_issue(0, Wd["f1g"], Wd["f1u"], Wd["f1d"])
                    ada_dma(4, adas)
                if piece == 3:
                    ada_dma(5, adas)
                    ffn_issue(1, Wd["f1g"], Wd["f1u"], Wd["f1d"])
            ada_scal(0, ("ab",))
            ada_keys = [("adapiece", 2), ("adapiece", 3)]
            load_x(xT, stg, [1], ada_keys)
            rr0 = {0: norm_stats(xtile(xT, 0), ("xT", 0))}
            for t in range(NT):
                if t + 2 < NT:
                    load_x(xT, stg, [t + 2], ada_keys)
                if t + 1 < NT:
                    rr0[t + 1] = norm_stats(xtile(xT, t + 1), ("xT", t + 1))
                norm_mod_tile(xtile(xT, t), ("xT", t), 0, t, rr0[t])

            def hook_dma(g):
                if g > 0:
                    ada_dma(6 + g, adas)
                if g == 10:
                    ada_dma(17, adas)

            def hook_pe(g, t):
                if g == 0 and t == 1:
                    ada_compute(4, adas)
                    ada_compute(5, adas)
                    ada_scal(0, ("g",))
                    ada_dma(6, adas)
                if t == 3:
                    ada_compute(6 + g, adas)
                if g == 10 and t == 4:
                    ada_compute(17, adas)
                if g == 7 and t == 0:
                    ada_scal(1)

            def hook1A(t):
                return norm_stats(xtile(xT, t), ("xT", t))

            def hook1B(t, r):
                norm_mod_tile(xtile(xT, t), ("xT", t), 1, t, r)
                S.dma("sp", scrX[:, :, t * TILE:(t + 1) * TILE], xT[:, :, t * TILE:(t + 1) * TILE],
                      reads=[("xT", t)], writes=[("scrX", t)])

            ffn(xT, 0, Wd["f1g"], Wd["f1u"], Wd["f1d"], actb, sgb, hook_dma, hook_pe, None, (4, 5, 6, 7), 2)
            ada_scal(2)
            issue_na_weights(0)
            if stop != 3:
                rr = {0: hook1A(0)}
                for t in range(NT):
                    if t + 1 < NT:
                        rr[t + 1] = hook1A(t + 1)
                    hook1B(t, rr[t])
            if stop == 2:
                return stop_here(xT, ffn_scope)
            if stop == 3:
                return stop_here(xT, ffn_scope)

            S.barrier()
            close_scope(ffn_scope)

            ys = new_scope()
            yT = sb("yT", [128, 8, T], BF16, ys)
            mx = new_scope()
            qT2 = sb("qT2", [128, 2, T], BF16, mx)
            ckT = sb("ckT", [128, 4, 256], BF16, mx)
            ckbT = sb("ckbT", [128, 2, 256], BF16, mx)
            cvA = sb("cvA", [128, 2, 4, 256], BF16, mx)
            cvB = sb("cvB", [128, 2, 2, 256], BF16, mx)
            cst = sb("cst", [128, 2, 512], F32, mx)
            PnT = [sb(f"PnT{i}", [128, 512], BF16, mx) for i in range(4)]
            ostd = [sb(f"ostd{i}", [128, 512], F32, mx) for i in range(2)]
            osts = [sb(f"osts{i}", [128, 512], F32, mx) for i in range(2)]
            kvst = [sb(f"kvst{i}", [128, 128], F32, mx) for i in range(2)]
            nas = new_scope()
            kT = sb("kT", [128, T], BF16, nas)
            vA = sb("vA", [128, 20, 256], BF16, nas)
            Gb0s = [sb(f"Gb0_{i}", [64, 2, 14, 2, 64], BF16, nas) for i in range(2)]
            Us = [sb(f"U{i}", [128, 2, 23, 64], BF16, nas) for i in range(2)]
            negm = sb("negm", [128, 64], F32, nas)

            S.dma("sp", negm[:], c_negm, writes=["negm"])
            S.op("act", L("activation", es[:], es[:], AF.Exp), reads=["es"], writes=["es"])
            for vt, nm in ((vA, "vA"), (cvA, "cvA"), (cvB, "cvB")):
                S.op("pool", L("memset", vt[:], 1.0), writes=[nm])
            S.op("pool", L("memset", qT2[:], 0.0), writes=["qT"])

            def prep_caches():
                for m in range(2):
                    S.dma("sp", cst[:, 0, :], cna_k[m * 128:(m + 1) * 128, :], writes=[("cst", 0)])
                    b = nextbank("tr", [6, 7])
                    for c in range(4):
                        S.op("pe", L("transpose", banks[b][:, c * 128:(c + 1) * 128],
                                                                   cst[:, 0, c * 128:(c + 1) * 128], ident_f[:]),
                             reads=[("cst", 0), "ident_f"], writes=[PK(b)])
                    S.op("dve", L("tensor_copy", ckT[:, :, m * 128:(m + 1) * 128],
                                                                  banks[b][:].rearrange("p (c n) -> p c n", c=4)),
                         reads=[PK(b)], writes=["ckT"])
                    S.dma("sp", cst[:, 1, :], cna_v[m * 128:(m + 1) * 128, :], writes=[("cst", 1)])
                    S.op("dve", L("tensor_copy", cvA[:, m, :, 64:192],
                                                             cst[:, 1, :].rearrange("p (c d) -> p c d", c=4)),
                         reads=[("cst", 1)], writes=["cvA"])
                    S.dma("sp", cst[:, 0, 0:128], csw_k[m * 128:(m + 1) * 128, :], writes=[("cst", 0)])
                    for kv in range(2):
                        for hh in range(2):
                            S.op("dve", L("tensor_copy", cst[:, 0, 128 + (kv * 2 + hh) * 64:128 + (kv * 2 + hh + 1) * 64],
                                cst[:, 0, kv * 64:(kv + 1) * 64]), reads=[("cst", 0)], writes=[("cst", 0)])
                    b = nextbank("tr", [6, 7])
                    for kv in range(2):
                        S.op("pe", L("transpose", banks[b][:, kv * 128:(kv + 1) * 128],
                                                                     cst[:, 0, 128 + kv * 128:256 + kv * 128], ident_f[:]),
                             reads=[("cst", 0), "ident_f"], writes=[PK(b)])
                    S.op("dve", L("tensor_copy", ckbT[:, :, m * 128:(m + 1) * 128],
                                                                  banks[b][:, 0:256].rearrange("p (c n) -> p c n", c=2)),
                         reads=[PK(b)], writes=["ckbT"])
                    S.dma("sp", cst[:, 1, 0:128], csw_v[m * 128:(m + 1) * 128, :], writes=[("cst", 1)])
                    for kv in range(2):
                        S.op("dve", L("tensor_copy", cvB[:, m, kv, 64:192].rearrange("p (r d) -> p r d", r=2),
                                      cst[:, 1, kv * 64:(kv + 1) * 64].unsqueeze(1).to_broadcast([128, 2, 64])),
                             reads=[("cst", 1)], writes=["cvB"])

            chk(5)

            def proj_fm(b, wv, col0, t, slotkey):
                sl = slice(t * TILE, (t + 1) * TILE)
                for k in range(NCH):
                    S.op("pe", L("matmul", banks[b][:], wv[:, k, col0:col0 + 128], hT[:, k, sl],
                                                       start=(k == 0), stop=(k == 7)),
                         reads=[slotkey, ("hT", t)], writes=[PK(b)])

            def proj_tm(b, wv, col0, ncols, tok0, slotkey, pcol0=0):
                for k in range(NCH):
                    S.op("pe", L("matmul", banks[b][:, pcol0:pcol0 + ncols], hT[:, k, tok0:tok0 + 128],
                                                       wv[:, k, col0:col0 + ncols], start=(k == 0), stop=(k == 7)),
                         reads=[slotkey, ("hT", tok0 // TILE), ("hT", min(NT - 1, (tok0 + 127) // TILE))],
                         writes=[PK(b)])

            SB = [0, 1, 2, 3]
            OB = [4, 5]

            def normalize_pair(units, dst_full):
                k = nextbank("ost", [0, 1])
                ncols = units[0].ncols
                for u in units:
                    hh, bo = u.hh, u.bo
                    dp = slice(64, 128) if hh == 0 else slice(0, 64)
                    sp_ = slice(0, 64) if hh == 0 else slice(64, 128)
                    if u.sink_h is None:
                        S.op("dve", L("tensor_copy", ostd[k][sp_, 0:ncols], banks[bo][dp, 0:ncols]), reads=[PK(bo)],
                             writes=[("ostd", k)])
                    else:
                        S.op("act", L("copy", ostd[k][sp_, 0:ncols], banks[bo][dp, 0:ncols]), reads=[PK(bo)],
                             writes=[("ostd", k)])
                    if u.sink_h is None:
                        S.op("dve", L("tensor_copy", osts[k][sp_, 0:ncols], banks[bo][sp_, 0:ncols]),
                             reads=[PK(bo)], writes=[("osts", k)])
                    else:
                        S.op("act", L("activation", osts[k][sp_, 0:ncols], banks[bo][sp_, 0:ncols], AF.Identity,
                                      bias=es[sp_, u.sink_h:u.sink_h + 1], scale=1.0),
                             reads=[PK(bo), "es"], writes=[("osts", k)])
                S.op("dve", L("reciprocal", osts[k][:, 0:ncols], osts[k][:, 0:ncols]),
                     reads=[("osts", k)], writes=[("osts", k)])
                S.op("pool", L("tensor_tensor", dst_full, ostd[k][:, 0:ncols], osts[k][:, 0:ncols], ALU.mult),
                     reads=[("ostd", k), ("osts", k)], writes=["yT"])

            LOOK = 3
            PNB = [0, 1, 2, 3]
            OB = [4, 5, 6, 7]

            class Unit:
                def __init__(self, hh, ncols, dst, sink_h):
                    self.hh, self.ncols, self.dst, self.sink_h = hh, ncols, dst, sink_h
                    self.bo = None
                    self.started = False

            def run_pipeline(stages):
                n = len(stages)
                pbs = [None] * n
                for i in range(n + LOOK):
                    if i < n:
                        st_ = stages[i]
                        bs = nextbank("sn", SB)
                        pb = nextbank("PnT", PNB)
                        st_["emitS"](bs)
                        S.op("act", L("activation", PnT[pb][:, 0:st_["N2"]], banks[bs][:, 0:st_["N2"]], AF.Exp,
                                      scale=st_["scale"]),
                             reads=[PK(bs)], writes=[("PnT", pb)])
                        pbs[i] = pb
                    k = i - LOOK
                    if k >= 0:
                        st_ = stages[k]
                        for hh, u in enumerate(st_["units"]):
                            if u.bo is None:
                                u.bo = nextbank("ob", OB)
                            st_["emitPV"](pbs[k], hh, u.bo, not u.started, st_["last"])
                            u.started = True
                        if st_["last"]:
                            normalize_pair(st_["units"], st_["units"][0].dst)

            def mm(out, lhsT, rhs, start, stop, reads, wkey):
                S.op("pe", L("matmul", out, lhsT, rhs, start=start, stop=stop, skip_group_check=True),
                     reads=reads, writes=[wkey])

            def prompt_stages(kfn, vfn, vkey, sinks, chunk, scale):
                units = tuple(Unit(hh, 512, yT[:, chunk, TS:TS + 512], sinks[hh]) for hh in range(2))
                out = []
                for s_ in range(2):
                    t0 = TS + s_ * 256
                    for m in range(2):
                        def emitS(bs, t0=t0, m=m):
                            mm(banks[bs][:], kfn(t0 + m * 128),
                               qT2[:, :, t0:t0 + 256], True, True, ["kT", "kT2", "qT"], PK(bs))

                        def emitPV(pb, hh, bo, start, stop, s_=s_, m=m):
                            mm(banks[bo][:, s_ * 256:(s_ + 1) * 256], vfn(16 + s_ * 2 + m, hh),
                               PnT[pb][:, hh * 256:(hh + 1) * 256], start, stop, [("PnT", pb), vkey], PK(bo))

                        out.append(dict(units=units, N2=512, scale=scale, emitS=emitS, emitPV=emitPV,
                                        last=(s_ == 1 and m == 1)))
                return out

            def ctx_stages(units, kc_fn, t, v_fn, vkey, scale):
                out = []
                for m in range(2):
                    for qh in range(2):
                        q0 = t * TILE + qh * 256

                        def emitS(bs, m=m, q0=q0):
                            mm(banks[bs][:], kc_fn(m), qT2[:, :, q0:q0 + 256],
                               True, True, ["ckT", "ckbT", "qT"], PK(bs))

                        def emitPV(pb, hh, bo, start, stop, m=m, qh=qh):
                            mm(banks[bo][:, qh * 256:(qh + 1) * 256], v_fn(m, hh), PnT[pb][:, hh * 256:(hh + 1) * 256],
                               start, stop, [("PnT", pb), vkey], PK(bo))

                        out.append(dict(units=units, N2=512, scale=scale, emitS=emitS, emitPV=emitPV, last=False))
                return out

            def na_plan(t):
                plan = []
                for j in range(16):
                    rows = []
                    for r in range(8 * t, 8 * t + 8):
                        rs = min(max(r - 4, 0), 24)
                        lo_in = rs <= 2 * j <= rs + 7
                        hi_in = rs <= 2 * j + 1 <= rs + 7
                        if not (lo_in or hi_in):
                            continue
                        a = 2 * j - r + 7
                        if lo_in and hi_in:
                            cands = [22 - a] + ([10 - a] if 3 <= a <= 9 else [])
                        elif lo_in:
                            assert a == 10
                            cands = [0]
                        else:
                            assert a == 2
                            cands = [8]
                        rows.append((r, cands))
                    if not rows:
                        continue
                    assert [r for r, _ in rows] == list(range(rows[0][0], rows[0][0] + len(rows)))
                    for c0 in range(0, len(rows), 4):
                        part = rows[c0:c0 + 4]
                        runs = []
                        for i, (r, cands) in enumerate(part):
                            if runs and (runs[-1][1] + runs[-1][2]) in cands:
                                runs[-1][2] += 1
                            else:
                                pick = cands[0]
                                if i + 1 < len(part):
                                    for cd in cands:
                                        if cd + 1 in part[i + 1][1]:
                                            pick = cd
                                            break
                                runs.append([i, pick, 1])
                        plan.append((j, part[0][0], len(part), runs))
                return plan

            NA_PLANS = [na_plan(t) for t in range(4)]

            def build_bias(c):
                U = Us[c % 2]
                Gb0 = Gb0s[c % 2]
                ukey = ("U", c % 2)
                gkey = ("Gb0", c % 2)
                for hh in range(2):
                    h = 2 * c + hh
                    for s2 in range(2):
                        S.dma(wq, Gb0[0:64, hh, :, s2, :],
                              bass.AP(scrP_h, (h * 15 + s2) * 127, [[1, 64], [127, 14], [1, 64]]),
                              reads=["scrP"], writes=[gkey])
                for hh in range(2):
                    for piece in range(2):
                        b = nextbank("tr", [6, 7])
                        for a7 in range(7):
                            a_ = 13 - (piece * 7 + a7)
                            S.op("pe", L("matmul", banks[b][:, a7 * 64:(a7 + 1) * 64],
                                         Gb0[0:64, hh, a_, :, :].rearrange("p s k -> p (s k)"), jrev_b[0:64, 0:64],
                                         start=True, stop=True, skip_group_check=True),
                                 reads=[gkey, "jrev_b"], writes=[PK(b)])
                        S.op("dve", L("tensor_tensor", U[:, hh, 9 + piece * 7:16 + piece * 7, :],
                                      banks[b][:, 0:448].rearrange("p (a q) -> p a q", q=64),
                                      negm[:].unsqueeze(1).to_broadcast([128, 7, 64]), ALU.add),
                             reads=[PK(b), "negm"], writes=[ukey])
                    S.op("dve", L("tensor_copy", U[:, hh, 1:8, :], U[:, hh, 13:20, :]), reads=[ukey], writes=[ukey])
                    S.op("dve", L("tensor_copy", U[:, hh, 0, :], U[:, hh, 12, :]), reads=[ukey], writes=[ukey])
                    S.op("dve", L("tensor_copy", U[:, hh, 8, :], U[:, hh, 20, :]), reads=[ukey], writes=[ukey])
                    S.op("dve", L("memset", U[64:128, hh, 0, :], NEG), reads=[ukey], writes=[ukey])
                    S.op("dve", L("memset", U[0:64, hh, 8, :], NEG), reads=[ukey], writes=[ukey])

            def issue_swa_weights(c):
                kv = c // 2
                slot = c % 2
                skey = ("wslot", slot)
                wv = wslot[slot][:, 0:5120].rearrange("p (k n) -> p k n", k=8)
                S.dma(wq, wv[:, :, 0:128], Wd["w_in"][:, 1536 + c * 128:1536 + (c + 1) * 128].rearrange(
                    "(k p) n -> p k n", p=128), writes=[skey])
                for hh in range(2):
                    S.dma(wq, wv[:, :, 256 + hh * 64:256 + (hh + 1) * 64],
                          Wd["w_in"][:, 2048 + kv * 64:2048 + (kv + 1) * 64].rearrange("(k p) n -> p k n", p=128),
                          writes=[skey])
                S.dma(wq, wv[:, :, 512:640], Wd["w_in"][:, 2176:2304].rearrange("(k p) n -> p k n", p=128),
                      writes=[skey])

            def issue_merge_weights(dc):
                slot = dc % 2
                skey = ("wslot", slot)
                wgv = wslot[slot][:, 0:2048].rearrange("p (k n) -> p k n", k=8)
                wbv = wslot[slot][:, 2048:3072].rearrange("p (k n) -> p k n", k=4)
                S.dma(wq, wgv[:, :, 0:128], Wd["w_in"][:, 2304 + dc * 128:2304 + (dc + 1) * 128].rearrange(
                    "(k p) n -> p k n", p=128), writes=[skey])
                S.dma(wq, wgv[:, :, 128:256], Wd["w_in"][:, 3328 + dc * 128:3328 + (dc + 1) * 128].rearrange(
                    "(k p) n -> p k n", p=128), writes=[skey])
                S.dma(wq, wbv[:, :, 0:128], Wd["wb_na"][:, dc * 128:(dc + 1) * 128].rearrange("(k p) n -> p k n", p=128),
                      writes=[skey])
                S.dma(wq, wbv[:, :, 128:256], Wd["wb_sw"][:, dc * 128:(dc + 1) * 128].rearrange("(k p) n -> p k n", p=128),
                      writes=[skey])

            for c in range(4):
                slot = c % 2
                wv = wslot[slot][:, 0:3072].rearrange("p (k n) -> p k n", k=8)
                skey = ("wslot", slot)
                if c > 0:
                    issue_na_weights(c)
                chk(6)
                chk(100 + 10 * c + 0)
                for t in range(NT):
                    b = nextbank("pj", [0, 1, 2, 3])
                    proj_fm(b, wv, 0, t, skey)
                    for hh in range(2):
                        S.op("act", L("activation", qT2[hh * 64:hh * 64 + 64, hh, t * TILE:(t + 1) * TILE],
                                      banks[b][hh * 64:hh * 64 + 64, :], AF.Copy, scale=0.125),
                             reads=[PK(b)], writes=["qT"])
                    b = nextbank("pj", [0, 1, 2, 3])
                    proj_fm(b, wv, 128, t, skey)
                    S.op("act", L("copy", kT[:, t * TILE:(t + 1) * TILE], banks[b][:]),
                         reads=[PK(b)], writes=["kT"])
                for blk in range(20):
                    b = nextbank("pj", [0, 1, 2, 3])
                    proj_tm(b, wv, 256, 128, blk * 128, skey)
                    S.op("act", L("copy", vA[:, blk, 64:192], banks[b][:, 0:128]),
                        reads=[PK(b)], writes=["vA"])
                    if blk >= 16:
                        ks = nextbank("kvst", [0, 1])
                        S.op("act", L("copy", kvst[ks][:], banks[b][:, 0:128]), reads=[PK(b)],
                             writes=[("kvst", ks)])
                        S.dma("sp", nv_na[(blk - 16) * 128:(blk - 15) * 128, c * 128:(c + 1) * 128], kvst[ks][:],
                              reads=[("kvst", ks)], final=True)
                        b2 = nextbank("pj", [0, 1, 2, 3])
                        proj_tm(b2, wv, 128, 128, blk * 128, skey)
                        ks = nextbank("kvst", [0, 1])
                        S.op("act", L("copy", kvst[ks][:], banks[b2][:, 0:128]), reads=[PK(b2)],
                             writes=[("kvst", ks)])
                        S.dma("sp", nk_na[(blk - 16) * 128:(blk - 15) * 128, c * 128:(c + 1) * 128], kvst[ks][:],
                              reads=[("kvst", ks)], final=True)
                chk(7)
                chk(100 + 10 * c + 1)
                if c == 0:
                    prep_caches()
                    build_bias(0)
                if c + 1 < 4:
                    build_bias(c + 1)
                U = Us[c % 2]
                ukey = ("U", c % 2)
                stages = []
                for t in range(4):
                    units = tuple(Unit(hh, 512, yT[:, c, t * TILE:(t + 1) * TILE], None)
                                  for hh in range(2))
                    stages += ctx_stages(units, lambda m: ckT[:, c, m * 128:(m + 1) * 128], t,
                                         lambda m, hh: cvA[:, m, c, hh * 128:hh * 128 + 128], "cvA", 1.0)
                    plan = NA_PLANS[t]
                    for pi, (j, r0, nr, runs) in enumerate(plan):
                        N = nr * 64
                        q0 = (r0 - 8 * t) * 64

                        def emitS(bs, j=j, r0=r0, N=N, runs=runs, U=U, ukey=ukey):
                            if len(runs) == 1:
                                off, idx0, n = runs[0]
                                mm(banks[bs][:, 0:2 * N], ident_b[:],
                                   U[:, :, idx0:idx0 + n, :].rearrange("p h a q -> p h (a q)"),
                                   True, False, [ukey, "ident_b"], PK(bs))
                            else:
                                first = True
                                for (off, idx0, n) in runs:
                                    for hh in range(2):
                                        mm(banks[bs][:, hh * N + off * 64:hh * N + (off + n) * 64], ident_b[:],
                                           U[:, hh, idx0:idx0 + n, :].rearrange("p a q -> p (a q)"),
                                           first, False, [ukey, "ident_b"], PK(bs))
                                        first = False
                            mm(banks[bs][:, 0:2 * N], kT[:, j * 128:(j + 1) * 128], qT2[:, :, r0 * 64:r0 * 64 + N],
                               False, True, ["kT", "qT"], PK(bs))

                        def emitPV(pb, hh, bo, start, stop, j=j, N=N, q0=q0):
                            mm(banks[bo][:, q0:q0 + N], vA[:, j, hh * 128:hh * 128 + 128], PnT[pb][:, hh * N:(hh + 1) * N],
                               start, stop, [("PnT", pb), "vA"], PK(bo))

                        stages.append(dict(units=units, N2=2 * N, scale=1.0, emitS=emitS, emitPV=emitPV,
                                           last=(pi == len(plan) - 1)))
                chk(8)
                chk(100 + 10 * c + 2)
                stages += prompt_stages(lambda tok: kT[:, tok:tok + 128],
                                        lambda blk, hh: vA[:, blk, hh * 128:hh * 128 + 128], "vA", (None, None), c, 1.0)
                run_pipeline(stages)
                chk(12)
                chk(100 + 10 * c + 3)

            chk(9)
            issue_swa_weights(0)
            S.barrier()
            close_scope(nas)
            sws = new_scope()
            kT = None
            kT2 = sb("kT2", [128, 2, T], BF16, sws)
            vB = sb("vB", [128, 20, 256], BF16, sws)
            bandR = sb("bandR", [128, 3, 128], BF16, sws)
            cosT = sb("cosT", [128, TS], F32, sws)
            sinT = sb("sinT", [128, TS], F32, sws)
            rt = [sb(f"rt{i}", [128, 512], F32, sws) for i in range(2)]
            S.dma(wq, bandR[:].rearrange("p s n -> p (s n)"), c_band, writes=["bandR"])
            S.dma("sp", cosT[:], c_cos, writes=["cosT"])
            S.dma("sp", sinT[:], c_sin, writes=["sinT"])
            S.op("pool", L("memset", vB[:], 1.0), writes=["vB"])
            for c in range(4):
                kv = c // 2
                slot = c % 2
                skey = ("wslot", slot)
                wv = wslot[slot][:, 0:5120].rearrange("p (k n) -> p k n", k=8)
                if c > 0:
                    issue_swa_weights(c)
                for base in (0, 256):
                    src = wv[:, :, base:base + 128].rearrange("p k (g j i) -> p k g j i", g=4, j=2)
                    dst = wv[:, :, base + 128:base + 256].rearrange("p k (g j i) -> p k g j i", g=4, j=2)
                    for j in range(2):
                        S.op("pool", L("tensor_copy", dst[:, :, :, j, :], src[:, :, :, 1 - j, :]),
                             reads=[skey], writes=[skey])

                def rope_proj(col0, t, dsts, dkey):
                    b1 = nextbank("pj", [0, 1, 2, 3])
                    proj_fm(b1, wv, col0, t, skey)
                    if t == 4:
                        for psl, dap in dsts:
                            S.op("dve", L("tensor_copy", dap, banks[b1][psl, :]), reads=[PK(b1)], writes=[dkey])
                        return
                    b2 = nextbank("pj", [0, 1, 2, 3])
                    proj_fm(b2, wv, col0 + 128, t, skey)
                    sl = slice(t * TILE, (t + 1) * TILE)
                    r1 = nextbank("rt", [0, 1])
                    S.op("dve", L("tensor_tensor", rt[r1][:], banks[b1][:], cosT[:, sl], ALU.mult),
                         reads=[PK(b1), "cosT"], writes=[("rt", r1)])
                    r2 = nextbank("rt", [0, 1])
                    S.op("dve", L("tensor_tensor", rt[r2][:], banks[b2][:], sinT[:, sl], ALU.mult),
                         reads=[PK(b2), "sinT"], writes=[("rt", r2)])
                    for psl, dap in dsts:
                        S.op("dve", L("tensor_tensor", dap, rt[r1][psl, :], rt[r2][psl, :], ALU.add),
                             reads=[("rt", r1), ("rt", r2)], writes=[dkey])

                if c % 2 == 0:
                    for blk in range(20):
                        b = nextbank("pj", [0, 1, 2, 3])
                        proj_tm(b, wv, 512, 128, blk * 128, skey)
                        for r_ in range(2):
                            S.op("act", L("copy", vB[:, blk, 64 + r_ * 64:128 + r_ * 64], banks[b][:, kv * 64:(kv + 1) * 64]),
                                 reads=[PK(b)], writes=["vB"])
                        if c == 0 and blk >= 16:
                            ks = nextbank("kvst", [0, 1])
                            S.op("act", L("copy", kvst[ks][:], banks[b][:, 0:128]), reads=[PK(b)],
                                 writes=[("kvst", ks)])
                            S.dma("sp", nv_sw[(blk - 16) * 128:(blk - 15) * 128, :], kvst[ks][:],
                                  reads=[("kvst", ks)], final=True)
                        if blk >= 16:
                            b2 = nextbank("pj", [0, 1, 2, 3])
                            proj_tm(b2, wv, 256, 64, blk * 128, skey)
                            ks = nextbank("kvst", [0, 1])
                            S.op("act", L("copy", kvst[ks][:, 0:64], banks[b2][:, 0:64]),
                                 reads=[PK(b2)], writes=[("kvst", ks)])
                            S.dma("sp", nk_sw[(blk - 16) * 128:(blk - 15) * 128, kv * 64:(kv + 1) * 64],
                                  kvst[ks][:, 0:64], reads=[("kvst", ks)], final=True)
                for t in range(NT):
                    rope_proj(0, t, [(slice(hh * 64, hh * 64 + 64), qT2[hh * 64:hh * 64 + 64, hh, t * TILE:(t + 1) * TILE])
                                     for hh in range(2)], "qT")
                    if c % 2 == 0:
                        rope_proj(256, t, [(slice(0, 128), kT2[:, kv, t * TILE:(t + 1) * TILE])], "kT2")
                stages = []
                for t in range(4):
                    units = tuple(Unit(hh, 512, yT[:, 4 + c, t * TILE:(t + 1) * TILE], 2 * c + hh)
                                  for hh in range(2))
                    stages += ctx_stages(units, lambda m: ckbT[:, kv, m * 128:(m + 1) * 128], t,
                                         lambda m, hh: cvB[:, m, kv, hh * 128:hh * 128 + 128], "cvB", 0.125)
                    mlist = list(range(max(0, 4 * t - 1), min(15, 4 * t + 4) + 1))
                    parts = []
                    for mb in mlist:
                        nlo = max(4 * t, mb - 1)
                        nhi = min(4 * t + 3, mb + 1)
                        for n0 in range(nlo, nhi + 1, 2):
                            parts.append((mb, n0, min(nhi, n0 + 1)))
                    for pi, (mb, nlo, nhi) in enumerate(parts):
                        N = (nhi - nlo + 1) * 128
                        q0 = (nlo - 4 * t) * 128

                        def emitS(bs, mb=mb, nlo=nlo, nhi=nhi, N=N):
                            first = True
                            for hh in range(2):
                                for n_ in range(nlo, nhi + 1):
                                    if n_ == mb:
                                        continue
                                    c0 = hh * N + (n_ - nlo) * 128
                                    mm(banks[bs][:, c0:c0 + 128], ident_b[:], bandR[:, n_ - mb + 1, :], first, False,
                                       ["bandR", "ident_b"], PK(bs))
                                    first = False
                            mm(banks[bs][:, 0:2 * N], kT2[:, kv, mb * 128:(mb + 1) * 128], qT2[:, :, nlo * 128:(nhi + 1) * 128],
                               first, True, ["kT2", "qT"], PK(bs))

                        def emitPV(pb, hh, bo, start, stop, mb=mb, N=N, q0=q0):
                            mm(banks[bo][:, q0:q0 + N], vB[:, mb, hh * 128:hh * 128 + 128], PnT[pb][:, hh * N:(hh + 1) * N],
                               start, stop, [("PnT", pb), "vB"], PK(bo))

                        stages.append(dict(units=units, N2=2 * N, scale=0.125, emitS=emitS, emitPV=emitPV,
                                           last=(pi == len(parts) - 1)))
                stages += prompt_stages(lambda tok, kv=kv: kT2[:, kv, tok:tok + 128],
                                        lambda blk, hh: vB[:, blk, hh * 128:hh * 128 + 128], "vB",
                                        (2 * c, 2 * c + 1), 4 + c, 0.125)
                run_pipeline(stages)

            chk(10)
            issue_merge_weights(0)
            S.barrier()
            close_scope(sws)
            close_scope(mx)
            mg = new_scope()
            mergedT = sb("mergedT", [128, 8, T], BF16, mg)
            sg1 = [sb(f"sg1_{i}", [128, 512], F32, mg) for i in range(2)]
            sg2 = [sb(f"sg2_{i}", [128, 512], F32, mg) for i in range(2)]
            xnt = [sb(f"xnt{i}", [128, NCH, 512], F32, mg) for i in range(2)]
            for dc in range(NCH):
                slot = dc % 2
                skey = ("wslot", slot)
                wgv = wslot[slot][:, 0:2048].rearrange("p (k n) -> p k n", k=8)
                wbv = wslot[slot][:, 2048:3072].rearrange("p (k n) -> p k n", k=4)
                if dc > 0:
                    issue_merge_weights(dc)
                for t in range(NT):
                    sl = slice(t * TILE, (t + 1) * TILE)
                    bg1 = nextbank("mg", [0, 1, 2, 3]); bg2 = nextbank("mg", [0, 1, 2, 3])
                    bb1 = nextbank("mg", [0, 1, 2, 3]); bb2 = nextbank("mg", [0, 1, 2, 3])
                    proj_fm(bg1, wgv, 0, t, skey)
                    proj_fm(bg2, wgv, 128, t, skey)
                    for (bb, off, ch0) in ((bb1, 0, 0), (bb2, 128, 4)):
                        for k in range(4):
                            S.op("pe", L("matmul", banks[bb][:], wbv[:, k, off:off + 128], yT[:, ch0 + k, sl], start=(k == 0), stop=(k == 3)),
                                reads=[skey, "yT"], writes=[PK(bb)])
                    i1 = nextbank("sg1", [0, 1]); i2 = nextbank("sg2", [0, 1])
                    S.op("act", L("activation", sg1[i1][:], banks[bg1][:], AF.Sigmoid),
                         reads=[PK(bg1)], writes=[("sg1", i1)])
                    S.op("act", L("activation", sg2[i2][:], banks[bg2][:], AF.Sigmoid),
                         reads=[PK(bg2)], writes=[("sg2", i2)])
                    S.op("dve", L("tensor_tensor", sg1[i1][:], banks[bb1][:], sg1[i1][:], ALU.mult),
                         reads=[PK(bb1), ("sg1", i1)], writes=[("sg1", i1)])
                    S.op("dve", L("tensor_tensor", sg2[i2][:], banks[bb2][:], sg2[i2][:], ALU.mult),
                         reads=[PK(bb2), ("sg2", i2)], writes=[("sg2", i2)])
                    S.op("dve", L("tensor_tensor", mergedT[:, dc, sl], sg1[i1][:], sg2[i2][:], ALU.add),
                        reads=[("sg1", i1), ("sg2", i2)], writes=[("mergedT", t)])
            chk(11)
            wov = []
            yflat = yT[:].rearrange("p c t -> p (c t)")
            for half in range(2):
                wv_ = yflat[:, half * 4096:(half + 1) * 4096].rearrange("p (k n) -> p k n", k=8)
                S.dma(wq, wv_, Wd["w_out"][:, half * 512:(half + 1) * 512].rearrange("(k p) n -> p k n", p=128),
                      reads=["yT"], writes=["yT"])
                wov.append(wv_)
            ffn_issue(0, Wd["f2g"], Wd["f2u"], Wd["f2d"])
            ffn_issue(1, Wd["f2g"], Wd["f2u"], Wd["f2d"])
            xnt = list(xnt) + [yflat[:, 8192:16384].bitcast(F32).rearrange("p (c n) -> p c n", c=8)]
            for t in range(NT):
                kind = 0 if t < 4 else 1
                sl = slice(t * TILE, (t + 1) * TILE)
                xb = nextbank("xnt", list(range(len(xnt))))
                S.dma("sp", xnt[xb][:], scrX[:, :, sl], reads=[("scrX", t)] + (["yT"] if xb == 2 else []),
                      writes=[("xnt", xb)])
                for dc in range(NCH):
                    half, dci = divmod(dc, 4)
                    b = nextbank("wo", [0, 1, 2, 3, 4, 5])
                    for k in range(8):
                        S.op("pe", L("matmul", banks[b][:], wov[half][:, k, dci * 128:(dci + 1) * 128], mergedT[:, k, sl],
                            start=(k == 0), stop=(k == 7)),
                            reads=["yT", ("mergedT", t)], writes=[PK(b)])
                    S.op("dve", L("scalar_tensor_tensor", xnt[xb][:, dc, :], banks[b][:], scal[:, 1, 2, dc, kind:kind + 1],
                                  xnt[xb][:, dc, :], ALU.mult, ALU.add),
                        reads=[PK(b), ("xnt", xb), ("scal", 1, 2)], writes=[("xnt", xb)])
                S.dma("sp", scrX[:, :, sl], xnt[xb][:], reads=[("xnt", xb)], writes=[("scrX", t)])
                norm_mod_tile(lambda c, xb=xb: xnt[xb][:, c, :], ("xnt", xb), 2, t)
            S.barrier()
            close_scope(mg)
            close_scope(ys)

            f2 = new_scope()
            xT = sb("xT2", [128, NCH, T], F32, f2)
            actb = [sb(f"actb2_{i}", [128, 2, 512], BF16, f2) for i in range(3)]
            sgb = [sb(f"sgb2_{i}", [128, 512], F32, f2) for i in range(2)]
            stg = [sb(f"stg2_{i}", [128, D], F32, f2) for i in range(4)]
            for t in range(NT):
                S.dma("sp", xT[:, :, t * TILE:(t + 1) * TILE], scrX[:, :, t * TILE:(t + 1) * TILE],
                      reads=[("scrX", t)], writes=[("xT", t)])
            if stop == 4:
                return stop_here(xT, f2)
            def hook2A(t):
                return norm_stats(xtile(xT, t), ("xT", t))

            def hook2B(t, r):
                norm_recip(r)
                sl = slice(t * TILE, (t + 1) * TILE)
                for c in range(NCH):
                    S.op("dve", L("scalar_tensor_tensor", xT[:, c, sl], xT[:, c, sl], gfin[:, c:c + 1], rstd[r][:], ALU.mult, ALU.mult),
                        reads=[("xT", t), ("rstd", r), "prm"], writes=[("xT", t)])

            def hook2C(t):
                for tb in range(4):
                    tok0 = t * TILE + tb * 128
                    so = nextbank("stg", [0, 1, 2, 3])
                    for half in range(2):
                        b = nextbank("trf", [0, 1, 2, 3, 4, 5])
                        for j in range(4):
                            c = half * 4 + j
                            S.op("pe", L("transpose", banks[b][:, j * 128:(j + 1) * 128], xT[:, c, tok0:tok0 + 128], ident_f[:]),
                                reads=[("xT", t), "ident_f"], writes=[PK(b)])
                        S.op("act", L("copy", stg[so][:, half * 512:(half + 1) * 512], banks[b][:]), reads=[PK(b)],
                             writes=[("stg", so)])
                    S.dma("sp", y_tok[tok0:tok0 + 128, :], stg[so][:], reads=[("stg", so)], writes=[("yout", tok0)],
                          final=True)

            ffn(xT, 2, Wd["f2g"], Wd["f2u"], Wd["f2d"], actb, sgb, None, None, None, (4, 5, 6, 7), 2)
            rr = {0: hook2A(0), 1: hook2A(1)}
            hook2B(0, rr[0])
            for t in range(NT):
                if t + 2 < NT:
                    rr[t + 2] = hook2A(t + 2)
                if t + 1 < NT:
                    hook2B(t + 1, rr[t + 1])
                hook2C(t)
            S.emit()
            close_scope(f2)

        except _Stop:
            S.emit()
            while open_scopes:
                close_scope(open_scopes[-1])
    return nc


def _consts():
    ident = np.eye(128, dtype=np.float32)
    jrev = np.zeros((128, 128), np.float32)
    for s in range(2):
        for k in range(64):
            jrev[s * 64 + k, s * 64 + 63 - k] = 1.0
    tok = np.arange(TS)
    row = (tok // 64).astype(np.float32)
    col = (tok % 64).astype(np.float32)
    freqs = np.power(np.float32(10000.0), -np.arange(16, dtype=np.float32) / np.float32(16)).astype(np.float32)
    cos = np.zeros((128, TS), np.float32)
    sin = np.zeros((128, TS), np.float32)
    for p in range(128):
        d = p % 64
        pos = row if d < 32 else col
        dd = d % 32
        f = freqs[dd % 16]
        ang = (pos * f).astype(np.float32)
        cos[p] = np.cos(ang)
        sin[p] = np.sin(ang) * (-1.0 if dd < 16 else 1.0)
    kj = np.arange(128)[:, None]
    qi = np.arange(128)[None, :]
    band = np.zeros((128, 3, 128), np.float32)
    band[:, 0, :] = np.where(kj <= qi, 0.0, NEG)
    band[:, 2, :] = np.where(kj >= qi, 0.0, NEG)
    qc = np.arange(64)
    cstart = np.clip(qc - 8, 0, 48)
    kc = np.arange(64)[:, None]
    ok = (kc >= cstart[None, :]) & (kc < cstart[None, :] + 16)
    negm = np.where(ok, 0.0, NEG).astype(np.float32)
    negm = np.concatenate([negm, negm], axis=0)
    return dict(c_ident=ident, c_jrev=jrev, c_cos=cos, c_sin=sin,
                c_band=np.ascontiguousarray(band.reshape(128, 384)), c_negm=np.ascontiguousarray(negm))


_NC_CACHE = {}


def kernel(**inputs):
    return _run(_prep(**inputs))


def _prep(x_prompt, x_sample, cache_na_k, cache_na_v, cache_swa_k, cache_swa_v, c, c_ctx,
          w_ada, b_ada, norm_ffn1, ffn1_w_gate, ffn1_w_up, ffn1_w_down, norm_mix, w_in,
          na_rel_bias, swa_sink, w_branch_na, w_branch_swa, w_out, norm_ffn2,
          ffn2_w_gate, ffn2_w_up, ffn2_w_down, norm_final):
    f = lambda a: np.ascontiguousarray(np.asarray(a, dtype=np.float32))
    x_prompt, x_sample = f(x_prompt), f(x_sample)
    shared = {
        "w_ada": f(w_ada)[0], "b_ada": f(b_ada)[0], "norm_ffn1": f(norm_ffn1)[0],
        "f1g": f(ffn1_w_gate)[0], "f1u": f(ffn1_w_up)[0], "f1d": f(ffn1_w_down)[0],
        "norm_mix": f(norm_mix)[0], "w_in": f(w_in)[0],
        "rel_bias": f(na_rel_bias)[0].reshape(120, 31), "sink": f(swa_sink)[0],
        "wb_na": f(w_branch_na)[0], "wb_sw": f(w_branch_swa)[0], "w_out": f(w_out)[0],
        "norm_ffn2": f(norm_ffn2)[0], "f2g": f(ffn2_w_gate)[0], "f2u": f(ffn2_w_up)[0], "f2d": f(ffn2_w_down)[0],
        "norm_final": f(norm_final),
    }
    shared.update(_consts())
    cc, cx = f(c), f(c_ctx)
    cnk, cnv, csk, csv = f(cache_na_k), f(cache_na_v), f(cache_swa_k), f(cache_swa_v)
    in_maps = []
    for i in range(8):
        m = dict(shared)
        m["x_tok"] = np.ascontiguousarray(np.concatenate(
            [x_sample[i], x_prompt[2 * i], x_prompt[2 * i + 1]], axis=0))
        m["cond"] = np.ascontiguousarray(np.stack([cc[i], cx], axis=0))
        m["cna_k"] = np.ascontiguousarray(cnk[i, 0].reshape(256, 512))
        m["cna_v"] = np.ascontiguousarray(cnv[i, 0].reshape(256, 512))
        m["csw_k"] = np.ascontiguousarray(csk[i, 0].reshape(256, 128))
        m["csw_v"] = np.ascontiguousarray(csv[i, 0].reshape(256, 128))
        in_maps.append(m)
    return in_maps


def _run(in_maps):
    if "nc" not in _NC_CACHE:
        _NC_CACHE["nc"] = build()
    res = run_bass_kernel_spmd(_NC_CACHE["nc"], in_maps, core_ids=list(range(8)))
    R = res.results
    y_sample = np.stack([R[i]["y_tok"][:TS] for i in range(8)], axis=0)
    y_prompt = np.stack([R[i // 2]["y_tok"][TS + (i % 2) * 256:TS + (i % 2 + 1) * 256] for i in range(16)], axis=0)

    def gather(name, nh):
        out = np.stack([R[i // 2][name][(i % 2) * 256:(i % 2 + 1) * 256] for i in range(16)], axis=0)
        return np.ascontiguousarray(out.reshape(16, 1, 256, nh, 64).astype(np.float32))

    return (np.ascontiguousarray(y_prompt.astype(np.float32)), np.ascontiguousarray(y_sample.astype(np.float32)),
            gather("nk_na", 8), gather("nv_na", 8), gather("nk_sw", 2), gather("nv_sw", 2))
```

```python
import numpy as np
from contextlib import ExitStack
import concourse.bass as bass
import concourse.mybir as mybir
from concourse.bass_utils import run_bass_kernel_spmd

F32 = mybir.dt.float32
BF16 = mybir.dt.bfloat16
AF = mybir.ActivationFunctionType
ALU = mybir.AluOpType

D = 1024
FF = 2816
NCH = 8
NF = 22
T = 2560
TS = 2048
TILE = 512
NT = 5
EPS = 1e-6
NEG = -30000.0
ENG_NAMES = ("pe", "act", "dve", "pool", "sp")


class _Stop(Exception):
    pass


class Sched:
    N_DMA_SEMS = 8
    limit = 0
    nrec = 0

    def _count(self):
        self.nrec += 1
        if self.limit and self.nrec >= self.limit:
            raise _Stop()

    def __init__(self, nc, stack):
        self.nc = nc
        self.ops = {e: [] for e in ENG_NAMES}
        self.known = {e: {} for e in ENG_NAMES}
        self.last_w = {}
        self.readers = {}
        self.sem = {e: stack.enter_context(nc.semaphore("prog_" + e)) for e in ("pe", "act", "dve", "pool")}
        self.dma_sems, self.dma_next, self.dma_target = {}, {}, {}
        for q in ("sp", "pool", "act"):
            self.dma_sems[q] = [stack.enter_context(nc.semaphore(f"dma_{q}_{i}")) for i in range(self.N_DMA_SEMS)]
            self.dma_next[q] = 0
            self.dma_target[q] = [0] * self.N_DMA_SEMS
        self.final_tokens = []

    def _need(self, eng, tok, waits):
        if tok is None:
            return
        if tok[0] == "eng":
            _, f, idx = tok
            if f == eng and eng == "pe":
                return
            key = ("eng", f)
            if self.known[eng].get(key, -1) >= idx:
                return
            self.known[eng][key] = idx
            waits.append(tok)
            self.ops[f][idx]["needed"] = True
        else:
            _, q, si, target = tok
            key = ("dma", q, si)
            if self.known[eng].get(key, -1) >= target:
                return
            self.known[eng][key] = target
            waits.append(tok)

    def _deps(self, eng, reads, writes):
        waits = []
        for r in reads:
            self._need(eng, self.last_w.get(r), waits)
        for w in writes:
            self._need(eng, self.last_w.get(w), waits)
            for t in self.readers.get(w, ()):
                self._need(eng, t, waits)
        return waits

    def _commit(self, tok, reads, writes):
        for r in reads:
            self.readers.setdefault(r, []).append(tok)
        for w in writes:
            self.last_w[w] = tok
            self.readers[w] = []

    def op(self, eng, fn, reads=(), writes=()):
        self._count()
        psr = [r for r in reads if isinstance(r, tuple) and r[0] == "ps"]
        if psr:
            reads = [r for r in reads if r not in psr]
            writes = list(writes) + [r for r in psr if r not in writes]
        waits = self._deps(eng, reads, writes)
        idx = len(self.ops[eng])
        self.ops[eng].append(dict(kind="op", fn=fn, waits=waits, needed=False))
        tok = ("eng", eng, idx)
        self._commit(tok, reads, writes)
        return tok

    def dma(self, q, out, in_, reads=(), writes=(), final=False, **kw):
        self._count()
        waits = self._deps(q, reads, writes)
        si = self.dma_next[q]
        self.dma_next[q] = (si + 1) % self.N_DMA_SEMS
        prev_target = self.dma_target[q][si]
        if prev_target > 0:
            self._need(q, ("dma", q, si, prev_target), waits)
        target = prev_target + 16
        self.dma_target[q][si] = target
        self.ops[q].append(dict(kind="dma", out=out, in_=in_, waits=waits, si=si, kw=kw, needed=False))
        tok = ("dma", q, si, target)
        self._commit(tok, reads, writes)
        if final:
            self.final_tokens.append(tok)
        return tok

    def barrier(self):
        toks = []
        for e in ("pe", "act", "dve", "pool"):
            for i in range(len(self.ops[e]) - 1, -1, -1):
                if self.ops[e][i]["kind"] == "op":
                    toks.append(("eng", e, i))
                    break
        for q in ("sp", "pool", "act"):
            for si in range(self.N_DMA_SEMS):
                if self.dma_target[q][si] > 0:
                    toks.append(("dma", q, si, self.dma_target[q][si]))
        for e in ENG_NAMES:
            waits = []
            for tk in toks:
                if tk[0] == "eng" and tk[1] == e:
                    continue
                self._need(e, tk, waits)
            if waits:
                self.ops[e].append(dict(kind="wait", waits=waits, needed=False))

    def emit(self):
        nc = self.nc
        fw = []
        for tok in self.final_tokens:
            self._need("sp", tok, fw)
        cum = {}
        for e in ("pe", "act", "dve", "pool"):
            c, arr = 0, []
            for o in self.ops[e]:
                if o["kind"] == "op" and o["needed"]:
                    c += 1
                arr.append(c)
            cum[e] = arr

        def emit_wait(engh, tok):
            if tok[0] == "eng":
                engh.wait_ge(self.sem[tok[1]], cum[tok[1]][tok[2]])
            else:
                engh.wait_ge(self.dma_sems[tok[1]][tok[2]], tok[3])

        def run(ename, engh):
            for o in self.ops[ename]:
                for tok in o["waits"]:
                    emit_wait(engh, tok)
                if o["kind"] == "op":
                    ins = o["fn"](engh)
                    if o["needed"]:
                        ins.then_inc(self.sem[ename], 1)
                elif o["kind"] == "dma":
                    engh.dma_start(out=o["out"], in_=o["in_"], **o["kw"]).then_inc(
                        self.dma_sems[ename][o["si"]], 16)
            if ename == "sp":
                for tok in fw:
                    emit_wait(engh, tok)

        with nc.Block() as block:
            @block.tensor
            def _(e):
                run("pe", e)

            @block.scalar
            def _(e):
                run("act", e)

            @block.vector
            def _(e):
                run("dve", e)

            @block.gpsimd
            def _(e):
                run("pool", e)

            @block.sync
            def _(e):
                run("sp", e)


def L(method, *a, **k):
    return lambda e: getattr(e, method)(*a, **k)


W_SPECS = [
    ("w_ada", [D, 9 * D]), ("b_ada", [9 * D]), ("norm_ffn1", [D]),
    ("f1g", [D, FF]), ("f1u", [D, FF]), ("f1d", [FF, D]),
    ("norm_mix", [D]), ("w_in", [D, 4352]), ("rel_bias", [120, 31]), ("sink", [8]),
    ("wb_na", [512, D]), ("wb_sw", [512, D]), ("w_out", [D, D]),
    ("norm_ffn2", [D]), ("f2g", [D, FF]), ("f2u", [D, FF]), ("f2d", [FF, D]), ("norm_final", [D]),
]


def build(stop=0):
    nc = bass.Bass("TRN2", target_bir_lowering=False)
    din = lambda name, shape: nc.dram_tensor(name, shape, F32, kind="ExternalInput").ap()
    dout = lambda name, shape: nc.dram_tensor(name, shape, F32, kind="ExternalOutput").ap()
    x_tok = din("x_tok", [T, D])
    cond = din("cond", [2, D])
    cna_k = din("cna_k", [256, 512]); cna_v = din("cna_v", [256, 512])
    csw_k = din("csw_k", [256, 128]); csw_v = din("csw_v", [256, 128])
    Wd = {n: din(n, s) for n, s in W_SPECS}
    c_ident = din("c_ident", [128, 128])
    c_jrev = din("c_jrev", [128, 128])
    c_cos = din("c_cos", [128, TS]); c_sin = din("c_sin", [128, TS])
    c_band = din("c_band", [128, 384])
    c_negm = din("c_negm", [128, 64])
    y_tok = dout("y_tok", [T, D])
    nk_na = dout("nk_na", [512, 512]); nv_na = dout("nv_na", [512, 512])
    nk_sw = dout("nk_sw", [512, 128]); nv_sw = dout("nv_sw", [512, 128])
    if stop:
        dbg = dout("dbg", [128, NCH, T])
        dbg2 = dout("dbg2", [128, 144])
    scrP_h = nc.dram_tensor("scrP", [120, 127], F32)
    scrX_h = nc.dram_tensor("scrX", [128, NCH, T], F32)
    scrX = scrX_h.ap()

    with ExitStack() as st:
        S = Sched(nc, st)
        if stop >= 1000:
            S.limit = stop

        def sb(name, shape, dt, stack=st):
            return stack.enter_context(nc.sbuf_tensor(name, shape, dt))

        open_scopes = []

        def new_scope():
            sc = ExitStack()
            open_scopes.append(sc)
            return sc

        def close_scope(sc):
            assert open_scopes[-1] is sc
            open_scopes.pop()
            sc.close()

        def chk(k):
            if stop == k:
                raise _Stop()

        try:
            banks = [st.enter_context(nc.psum_tensor(f"bank{i}", [128, 512], F32)) for i in range(8)]
            PK = lambda b: ("ps", b)
            rot = {}

            def nextbank(group, ids):
                i = rot.get(group, 0)
                rot[group] = i + 1
                return ids[i % len(ids)]

            hT = sb("hT", [128, NCH, T], BF16)
            wslot = [sb(f"wslot{i}", [128, 6144], BF16) for i in range(2)]
            ident_f = sb("ident_f", [128, 128], F32)
            ident_b = sb("ident_b", [128, 128], BF16)
            jrev_b = sb("jrev_b", [128, 128], BF16)
            ones_b = sb("ones_b", [128, 128], BF16)
            scal = sb("scal", [128, 3, 3, NCH, 2], F32)
            prm = sb("prm", [128, 120], F32)
            prow = sb("prow", [120, 128], F32)
            badaT = prm[:, 0:72]
            gains = prm[:, 72:96].rearrange("p (i c) -> p i c", i=3)
            gfin = prm[:, 96:104]
            condT = prm[:, 104:120].rearrange("p (k c) -> p c k", k=2)
            modT = sb("modT", [128, 72, 2], F32)
            scT = sb("scT", [128, NCH, 2], BF16)
            es = sb("es", [128, 8], F32)
            sqb = [sb(f"sqb{i}", [128, 512], BF16) for i in range(4)]
            rstd = [sb(f"rstd{i}", [128, 512], F32) for i in range(3)]
            tmpn = [sb(f"tmpn{i}", [128, 512], F32) for i in range(4)]
            zP = sb("zP", [120, 127], F32)

            wq = "pool"

            S.dma("sp", ident_f[:], c_ident, writes=["ident_f"])
            S.dma(wq, ident_b[:], c_ident, writes=["ident_b"])
            S.dma(wq, jrev_b[:], c_jrev, writes=["jrev_b"])
            S.op("dve", L("memset", ones_b[:], 1.0 / D), writes=["ones_b"])
            S.dma("sp", prow[0:72, :], Wd["b_ada"].rearrange("(j p) -> j p", p=128), writes=["prow"])
            for i, nm in enumerate(("norm_ffn1", "norm_mix", "norm_ffn2", "norm_final")):
                S.dma("sp", prow[72 + 8 * i:80 + 8 * i, :], Wd[nm].rearrange("(c p) -> c p", p=128), writes=["prow"])
            for k_ in range(2):
                S.dma("sp", prow[104 + 8 * k_:112 + 8 * k_, :], cond[k_, :].rearrange("(c p) -> c p", p=128), writes=["prow"])
            S.op("pe", L("transpose", banks[7][:, 0:120], prow[:, :], ident_f[0:120, 0:120]), reads=["prow", "ident_f"],
                 writes=[PK(7)])
            S.op("dve", L("tensor_copy", prm[:], banks[7][:, 0:120]), reads=[PK(7)], writes=["prm"])
            def late_consts():
                S.dma("sp", es[:], bass.AP(Wd["sink"].tensor, 0, [[0, 128], [1, 8]]), writes=["es"])
                S.op("dve", L("memset", zP[:], 0.0), writes=["zP"])
                S.dma("sp", zP[:, 48:79], Wd["rel_bias"], writes=["zP"], reads=["zP"])
                S.dma("sp", scrP_h.ap(), zP[:], reads=["zP"], writes=["scrP"])

            def load_x(xT, stg, tiles=range(NT), after=()):
                ns = len(stg)
                for tb in [4 * t_ + i_ for t_ in tiles for i_ in range(4)]:
                    xs = stg[tb % ns]
                    S.dma("sp", xs[:], x_tok[tb * 128:(tb + 1) * 128, :], reads=list(after), writes=[("xs", tb % ns)])
                    for half in range(2):
                        b = nextbank("trx", [0, 1, 2, 3, 4, 5])
                        for j in range(4):
                            c = half * 4 + j
                            S.op("pe", L("transpose", banks[b][:, j * 128:(j + 1) * 128], xs[:, c * 128:(c + 1) * 128], ident_f[:]),
                                reads=[("xs", tb % ns), "ident_f"], writes=[PK(b)])
                        dst = xT[:, half * 4:half * 4 + 4, tb * 128:(tb + 1) * 128]
                        src = banks[b][:].rearrange("p (j n) -> p j n", j=4)
                        if half == 0:
                            S.op("act", L("copy", dst, src), reads=[PK(b)],
                                 writes=[("xT", tb // 4)])
                        else:
                            S.op("dve", L("tensor_copy", dst, src), reads=[PK(b)],
                                 writes=[("xT", tb // 4)])

            ada_bank = 7

            def ada_buf(piece, adas):
                if piece in (2, 3):
                    return wslot[piece % 2][:, 0:4096], ("wslot", piece % 2)
                return adas[piece % 2][:], ("adas", piece % 2)

            def ada_dma(piece, adas):
                buf, key = ada_buf(piece, adas)
                wv = buf.rearrange("p (c n) -> p c n", c=8)
                S.dma(wq, wv, Wd["w_ada"][:, piece * 512:(piece + 1) * 512].rearrange("(c p) n -> p c n", p=128),
                      writes=[key, ("adapiece", piece)])

            def ada_compute(piece, adas):
                buf, key = ada_buf(piece, adas)
                slot = key
                wv = buf.rearrange("p (c n) -> p c n", c=8)
                b = nextbank("dn", [4, 5, 6, 7])
                pm = banks[b][:, 0:8].rearrange("p (j k) -> p j k", k=2)
                for jj in range(4):
                    for k in range(8):
                        S.op("pe", L("matmul", pm[:, jj, :], wv[:, k, jj * 128:(jj + 1) * 128], scT[:, k, :],
                            start=(k == 0), stop=(k == 7), skip_group_check=True),
                            reads=[slot, "scT"], writes=[PK(b)])
                j0 = piece * 4
                S.op("dve", L("tensor_tensor", modT[:, j0:j0 + 4, :], pm,
                              badaT[:, j0:j0 + 4].unsqueeze(2).to_broadcast([128, 4, 2]), ALU.add),
                     reads=[PK(b), "prm"], writes=["modT"])

            def ada_scal(ph, parts=("ab", "g")):
                sh = modT[:, (3 * ph) * 8:(3 * ph + 1) * 8, :]
                sc = modT[:, (3 * ph + 1) * 8:(3 * ph + 2) * 8, :]
                gt = modT[:, (3 * ph + 2) * 8:(3 * ph + 3) * 8, :]
                gb = gains[:, ph, :].unsqueeze(2).to_broadcast([128, 8, 2])
                if "ab" in parts:
                    S.op("dve", L("scalar_tensor_tensor", scal[:, ph, 0, :, :], sc, 1.0, gb, ALU.add, ALU.mult),
                        reads=["modT", "prm"], writes=[("scal", ph, 0)])
                    S.op("dve", L("tensor_copy", scal[:, ph, 1, :, :], sh),
                         reads=["modT"], writes=[("scal", ph, 1)])
                if "g" in parts:
                    S.op("dve", L("tensor_scalar", scal[:, ph, 2, :, :], gt, 0.5 if ph != 1 else 1.0, None, ALU.mult),
                        reads=["modT"], writes=[("scal", ph, 2)])

            def norm_stats(xsrc, xkey):
                b = 6 if (rot.get("nrm", 0) % 2 == 0) else 7
                rot["nrm"] = rot.get("nrm", 0) + 1
                for c in range(NCH):
                    q = nextbank("sq", [0, 1, 2, 3])
                    S.op("act", L("activation", sqb[q][:], xsrc(c), AF.Square),
                         reads=[xkey], writes=[("sqb", q)])
                    S.op("pe", L("matmul", banks[b][:], ones_b[:], sqb[q][:],
                                                                 start=(c == 0), stop=(c == 7)),
                         reads=[("sqb", q), "ones_b"], writes=[PK(b)])
                r = nextbank("rstd", [0, 1, 2])
                S.op("act", L("activation", rstd[r][:], banks[b][:], AF.Sqrt, bias=EPS, scale=1.0),
                     reads=[PK(b)], writes=[("rstd", r)])
                return r

            def norm_recip(r):
                S.op("dve", L("reciprocal", rstd[r][:], rstd[r][:]),
                     reads=[("rstd", r)], writes=[("rstd", r)])

            def norm_mod_tile(xsrc, xkey, ph, t, r=None):
                kind = 0 if t < 4 else 1
                if r is None:
                    r = norm_stats(xsrc, xkey)
                norm_recip(r)
                sl = slice(t * TILE, (t + 1) * TILE)
                for c in range(NCH):
                    q = nextbank("tmpn", [0, 1, 2, 3])
                    if True:
                        S.op("dve", L("scalar_tensor_tensor", tmpn[q][:], xsrc(c), scal[:, ph, 0, c, kind:kind + 1], rstd[r][:], ALU.mult, ALU.mult),
                            reads=[xkey, ("rstd", r), ("scal", ph, 0)], writes=[("tmpn", q)])
                        S.op("act", L("activation", hT[:, c, sl], tmpn[q][:], AF.Identity, bias=scal[:, ph, 1, c, kind:kind + 1], scale=1.0),
                            reads=[("tmpn", q), ("scal", ph, 1)], writes=[("hT", t)])
                    else:
                        S.op("pool", L("tensor_tensor", tmpn[q][:], xsrc(c), rstd[r][:], ALU.mult),
                            reads=[xkey, ("rstd", r)], writes=[("tmpn", q)])
                        S.op("act", L("activation", hT[:, c, sl], tmpn[q][:], AF.Identity, bias=scal[:, ph, 1, c, kind:kind + 1],
                                      scale=scal[:, ph, 0, c, kind:kind + 1]),
                            reads=[("tmpn", q), ("scal", ph, 1), ("scal", ph, 0)], writes=[("hT", t)])

            def xtile(xT, t):
                return lambda c: xT[:, c, t * TILE:(t + 1) * TILE]

            def issue_na_weights(c):
                slot = c % 2
                wv = wslot[slot][:, 0:3072].rearrange("p (k n) -> p k n", k=8)
                for i, base in enumerate((0, 512, 1024)):
                    S.dma(wq, wv[:, :, i * 128:(i + 1) * 128],
                          Wd["w_in"][:, base + c * 128:base + (c + 1) * 128].rearrange("(k p) n -> p k n", p=128),
                          writes=[("wslot", slot)])

            def ffn_issue(g, wg, wu, wdn):
                G = 2
                slot = g % 2
                f0 = g * G * 128
                wgv = wslot[slot][:, 0:2048].rearrange("p (c n) -> p c n", c=8)
                wuv = wslot[slot][:, 2048:4096].rearrange("p (c n) -> p c n", c=8)
                wdv = wslot[slot][:, 4096:6144].rearrange("p (j n) -> p j n", j=G)
                S.dma(wq, wgv, wg[:, f0:f0 + G * 128].rearrange("(c p) n -> p c n", p=128), writes=[("wslot", slot)])
                S.dma(wq, wuv, wu[:, f0:f0 + G * 128].rearrange("(c p) n -> p c n", p=128), writes=[("wslot", slot)])
                S.dma(wq, wdv, wdn[f0:f0 + G * 128, :].rearrange("(j p) n -> p j n", p=128), writes=[("wslot", slot)])

            def ffn(xT, ph, wg, wu, wdn, actb, sgb, hook_dma=None, hook_pe=None, hook_final=None, dnb=(4, 5, 6), pre=0):
                G = 2
                ngrp = NF // G
                pend = None
                hstate = []

                def emit_down(g, t, slot, ab, dcs):
                    kind = 0 if t < 4 else 1
                    wdv = wslot[slot][:, 4096:6144].rearrange("p (j n) -> p j n", j=G)
                    sl = slice(t * TILE, (t + 1) * TILE)
                    for dc in dcs:
                        b = nextbank("dn", list(dnb))
                        for j in range(G):
                            S.op("pe", L("matmul", banks[b][:], wdv[:, j, dc * 128:(dc + 1) * 128], actb[ab][:, j, :],
                                start=(j == 0), stop=(j == G - 1)),
                                reads=[("wslot", slot), ("actb", ab)], writes=[PK(b)])
                        S.op("dve", L("scalar_tensor_tensor", xT[:, dc, sl], banks[b][:], scal[:, ph, 2, dc, kind:kind + 1], xT[:, dc, sl], ALU.mult, ALU.add),
                            reads=[PK(b), ("xT", t), ("scal", ph, 2)], writes=[("xT", t)])

                for g in range(ngrp):
                    slot = g % 2
                    f0 = g * G * 128
                    wgv = wslot[slot][:, 0:2048].rearrange("p (c n) -> p c n", c=8)
                    wuv = wslot[slot][:, 2048:4096].rearrange("p (c n) -> p c n", c=8)
                    wdv = wslot[slot][:, 4096:6144].rearrange("p (j n) -> p j n", j=G)
                    if g >= pre:
                        ffn_issue(g, wg, wu, wdn)
                    if hook_dma is not None:
                        hook_dma(g)
                    for t in range(NT):
                        if hook_pe is not None:
                            hook_pe(g, t)
                        ab = nextbank("actb", [0, 1, 2])
                        sl = slice(t * TILE, (t + 1) * TILE)
                        for j in range(G):
                            bg = nextbank("gu", [0, 1, 2, 3])
                            bu = nextbank("gu", [0, 1, 2, 3])
                            for (bb, wv) in ((bg, wgv), (bu, wuv)):
                                for c in range(NCH):
                                    S.op("pe", L("matmul", banks[bb][:], wv[:, c, j * 128:(j + 1) * 128], hT[:, c, sl],
                                        start=(c == 0), stop=(c == 7)),
                                        reads=[("wslot", slot), ("hT", t)], writes=[PK(bb)])
                            sq_ = nextbank("sgb", [0, 1])
                            S.op("act", L("activation", sgb[sq_][:], banks[bg][:], AF.Silu),
                                 reads=[PK(bg)], writes=[("sgb", sq_)])
                            S.op("dve", L("tensor_tensor", actb[ab][:, j, :], banks[bu][:], sgb[sq_][:], ALU.mult),
                                reads=[PK(bu), ("sgb", sq_)], writes=[("actb", ab)])
                            if pend is not None:
                                emit_down(*pend, range(j * 4, j * 4 + 4))
                                if j == G - 1 and hook_final is not None and pend[0] == ngrp - 1:
                                    if hstate:
                                        hook_final[1](*hstate.pop())
                                    hstate.append((pend[1], hook_final[0](pend[1])))
                        pend = (g, t, slot, ab)
                emit_down(*pend, range(8))
                if hook_final is not None:
                    if hstate:
                        hook_final[1](*hstate.pop())
                    tl = pend[1]
                    hook_final[1](tl, hook_final[0](tl))

            ffn_scope = new_scope()
            xT = sb("xT", [128, NCH, T], F32, ffn_scope)
            actb = [sb(f"actb{i}", [128, 2, 512], BF16, ffn_scope) for i in range(3)]
            sgb = [sb(f"sgb{i}", [128, 512], F32, ffn_scope) for i in range(2)]
            stg = [sb(f"stg{i}", [128, D], F32, ffn_scope) for i in range(3)]
            def stop_here(xsrc, scope):
                if xsrc is not None:
                    S.dma("sp", dbg, xsrc[:], reads=[("xT", t_) for t_ in range(NT)], final=True)
                S.dma("sp", dbg2, modT[:].rearrange("p j k -> p (j k)"), reads=["modT"], final=True)
                S.emit()
                close_scope(scope)
                return nc

            adas = [sb(f"adas{i}", [128, 4096], BF16, ffn_scope) for i in range(2)]
            S.op("act", L("activation", scT[:], condT[:], AF.Silu), reads=["prm"], writes=["scT"])
            for piece in range(4):
                ada_dma(piece, adas)
            load_x(xT, stg, [0])
            late_consts()
            if stop == 1:
                load_x(xT, stg, [1, 2, 3, 4])
                S.op("dve", L("memset", modT[:], 0.0), writes=["modT"])
                return stop_here(xT, ffn_scope)
            for piece in range(4):
                ada_compute(piece, adas)
                if piece == 2:
                    ffn_issue(0, Wd["f1g"], Wd["f1u"], Wd["f1d"])
                    ada_dma(4, adas)
                if piece == 3:
                    ada_dma(5, adas)
                    ffn_issue(1, Wd["f1g"], Wd["f1u"], Wd["f1d"])
            ada_scal(0, ("ab",))
            ada_keys = [("adapiece", 2), ("adapiece", 3)]
            load_x(xT, stg, [1], ada_keys)
            rr0 = {0: norm_stats(xtile(xT, 0), ("xT", 0))}
            for t in range(NT):
                if t + 2 < NT:
                    load_x(xT, stg, [t + 2], ada_keys)
                if t + 1 < NT:
                    rr0[t + 1] = norm_stats(xtile(xT, t + 1), ("xT", t + 1))
                norm_mod_tile(xtile(xT, t), ("xT", t), 0, t, rr0[t])

            def hook_dma(g):
                if g > 0:
                    ada_dma(6 + g, adas)
                if g == 10:
                    ada_dma(17, adas)

            def hook_pe(g, t):
                if g == 0 and t == 1:
                    ada_compute(4, adas)
                    ada_compute(5, adas)
                    ada_scal(0, ("g",))
                    ada_dma(6, adas)
                if t == 3:
                    ada_compute(6 + g, adas)
                if g == 10 and t == 4:
                    ada_compute(17, adas)
                if g == 7 and t == 0:
                    ada_scal(1)

            def hook1A(t):
                return norm_stats(xtile(xT, t), ("xT", t))

            def hook1B(t, r):
                norm_mod_tile(xtile(xT, t), ("xT", t), 1, t, r)
                S.dma("sp", scrX[:, :, t * TILE:(t + 1) * TILE], xT[:, :, t * TILE:(t + 1) * TILE],
                      reads=[("xT", t)], writes=[("scrX", t)])

            ffn(xT, 0, Wd["f1g"], Wd["f1u"], Wd["f1d"], actb, sgb, hook_dma, hook_pe, None, (4, 5, 6, 7), 2)
            ada_scal(2)
            issue_na_weights(0)
            if stop != 3:
                rr = {0: hook1A(0)}
                for t in range(NT):
                    if t + 1 < NT:
                        rr[t + 1] = hook1A(t + 1)
                    hook1B(t, rr[t])
            if stop == 2:
                return stop_here(xT, ffn_scope)
            if stop == 3:
                return stop_here(xT, ffn_scope)

            S.barrier()
            close_scope(ffn_scope)

            ys = new_scope()
            yT = sb("yT", [128, 8, T], BF16, ys)
            mx = new_scope()
            qT2 = sb("qT2", [128, 2, T], BF16, mx)
            ckT = sb("ckT", [128, 4, 256], BF16, mx)
            ckbT = sb("ckbT", [128, 2, 256], BF16, mx)
            cvA = sb("cvA", [128, 2, 4, 256], BF16, mx)
            cvB = sb("cvB", [128, 2, 2, 256], BF16, mx)
            cst = sb("cst", [128, 2, 512], F32, mx)
            PnT = [sb(f"PnT{i}", [128, 512], BF16, mx) for i in range(4)]
            ostd = [sb(f"ostd{i}", [128, 512], F32, mx) for i in range(2)]
            osts = [sb(f"osts{i}", [128, 512], F32, mx) for i in range(2)]
            kvst = [sb(f"kvst{i}", [128, 128], F32, mx) for i in range(2)]
            nas = new_scope()
            kT = sb("kT", [128, T], BF16, nas)
            vA = sb("vA", [128, 20, 256], BF16, nas)
            Gb0s = [sb(f"Gb0_{i}", [64, 2, 14, 2, 64], BF16, nas) for i in range(2)]
            Us = [sb(f"U{i}", [128, 2, 23, 64], BF16, nas) for i in range(2)]
            negm = sb("negm", [128, 64], F32, nas)

            S.dma("sp", negm[:], c_negm, writes=["negm"])
            S.op("act", L("activation", es[:], es[:], AF.Exp), reads=["es"], writes=["es"])
            for vt, nm in ((vA, "vA"), (cvA, "cvA"), (cvB, "cvB")):
                S.op("pool", L("memset", vt[:], 1.0), writes=[nm])
            S.op("pool", L("memset", qT2[:], 0.0), writes=["qT"])

            def prep_caches():
                for m in range(2):
                    S.dma("sp", cst[:, 0, :], cna_k[m * 128:(m + 1) * 128, :], writes=[("cst", 0)])
                    b = nextbank("tr", [6, 7])
                    for c in range(4):
                        S.op("pe", L("transpose", banks[b][:, c * 128:(c + 1) * 128],
                                                                   cst[:, 0, c * 128:(c + 1) * 128], ident_f[:]),
                             reads=[("cst", 0), "ident_f"], writes=[PK(b)])
                    S.op("dve", L("tensor_copy", ckT[:, :, m * 128:(m + 1) * 128],
                                                                  banks[b][:].rearrange("p (c n) -> p c n", c=4)),
                         reads=[PK(b)], writes=["ckT"])
                    S.dma("sp", cst[:, 1, :], cna_v[m * 128:(m + 1) * 128, :], writes=[("cst", 1)])
                    S.op("dve", L("tensor_copy", cvA[:, m, :, 64:192],
                                                             cst[:, 1, :].rearrange("p (c d) -> p c d", c=4)),
                         reads=[("cst", 1)], writes=["cvA"])
                    S.dma("sp", cst[:, 0, 0:128], csw_k[m * 128:(m + 1) * 128, :], writes=[("cst", 0)])
                    for kv in range(2):
                        for hh in range(2):
                            S.op("dve", L("tensor_copy", cst[:, 0, 128 + (kv * 2 + hh) * 64:128 + (kv * 2 + hh + 1) * 64],
                                cst[:, 0, kv * 64:(kv + 1) * 64]), reads=[("cst", 0)], writes=[("cst", 0)])
                    b = nextbank("tr", [6, 7])
                    for kv in range(2):
                        S.op("pe", L("transpose", banks[b][:, kv * 128:(kv + 1) * 128],
                                                                     cst[:, 0, 128 + kv * 128:256 + kv * 128], ident_f[:]),
                             reads=[("cst", 0), "ident_f"], writes=[PK(b)])
                    S.op("dve", L("tensor_copy", ckbT[:, :, m * 128:(m + 1) * 128],
                                                                  banks[b][:, 0:256].rearrange("p (c n) -> p c n", c=2)),
                         reads=[PK(b)], writes=["ckbT"])
                    S.dma("sp", cst[:, 1, 0:128], csw_v[m * 128:(m + 1) * 128, :], writes=[("cst", 1)])
                    for kv in range(2):
                        S.op("dve", L("tensor_copy", cvB[:, m, kv, 64:192].rearrange("p (r d) -> p r d", r=2),
                                      cst[:, 1, kv * 64:(kv + 1) * 64].unsqueeze(1).to_broadcast([128, 2, 64])),
                             reads=[("cst", 1)], writes=["cvB"])

            chk(5)

            def proj_fm(b, wv, col0, t, slotkey):
                sl = slice(t * TILE, (t + 1) * TILE)
                for k in range(NCH):
                    S.op("pe", L("matmul", banks[b][:], wv[:, k, col0:col0 + 128], hT[:, k, sl],
                                                       start=(k == 0), stop=(k == 7)),
                         reads=[slotkey, ("hT", t)], writes=[PK(b)])

            def proj_tm(b, wv, col0, ncols, tok0, slotkey, pcol0=0):
                for k in range(NCH):
                    S.op("pe", L("matmul", banks[b][:, pcol0:pcol0 + ncols], hT[:, k, tok0:tok0 + 128],
                                                       wv[:, k, col0:col0 + ncols], start=(k == 0), stop=(k == 7)),
                         reads=[slotkey, ("hT", tok0 // TILE), ("hT", min(NT - 1, (tok0 + 127) // TILE))],
                         writes=[PK(b)])

            SB = [0, 1, 2, 3]
            OB = [4, 5]

            def normalize_pair(units, dst_full):
                k = nextbank("ost", [0, 1])
                ncols = units[0].ncols
                for u in units:
                    hh, bo = u.hh, u.bo
                    dp = slice(64, 128) if hh == 0 else slice(0, 64)
                    sp_ = slice(0, 64) if hh == 0 else slice(64, 128)
                    if u.sink_h is None:
                        S.op("dve", L("tensor_copy", ostd[k][sp_, 0:ncols], banks[bo][dp, 0:ncols]), reads=[PK(bo)],
                             writes=[("ostd", k)])
                    else:
                        S.op("act", L("copy", ostd[k][sp_, 0:ncols], banks[bo][dp, 0:ncols]), reads=[PK(bo)],
                             writes=[("ostd", k)])
                    if u.sink_h is None:
                        S.op("dve", L("tensor_copy", osts[k][sp_, 0:ncols], banks[bo][sp_, 0:ncols]),
                             reads=[PK(bo)], writes=[("osts", k)])
                    else:
                        S.op("act", L("activation", osts[k][sp_, 0:ncols], banks[bo][sp_, 0:ncols], AF.Identity,
                                      bias=es[sp_, u.sink_h:u.sink_h + 1], scale=1.0),
                             reads=[PK(bo), "es"], writes=[("osts", k)])
                S.op("dve", L("reciprocal", osts[k][:, 0:ncols], osts[k][:, 0:ncols]),
                     reads=[("osts", k)], writes=[("osts", k)])
                S.op("dve", L("tensor_tensor", dst_full, ostd[k][:, 0:ncols], osts[k][:, 0:ncols], ALU.mult),
                     reads=[("ostd", k), ("osts", k)], writes=["yT"])

            LOOK = 3
            PNB = [0, 1, 2, 3]
            OB = [4, 5, 6, 7]

            class Unit:
                def __init__(self, hh, ncols, dst, sink_h):
                    self.hh, self.ncols, self.dst, self.sink_h = hh, ncols, dst, sink_h
                    self.bo = None
                    self.started = False

            def run_pipeline(stages):
                n = len(stages)
                pbs = [None] * n
                for i in range(n + LOOK):
                    if i < n:
                        st_ = stages[i]
                        bs = nextbank("sn", SB)
                        pb = nextbank("PnT", PNB)
                        st_["emitS"](bs)
                        S.op("act", L("activation", PnT[pb][:, 0:st_["N2"]], banks[bs][:, 0:st_["N2"]], AF.Exp,
                                      scale=st_["scale"]),
                             reads=[PK(bs)], writes=[("PnT", pb)])
                        pbs[i] = pb
                    k = i - LOOK
                    if k >= 0:
                        st_ = stages[k]
                        for hh, u in enumerate(st_["units"]):
                            if u.bo is None:
                                u.bo = nextbank("ob", OB)
                            st_["emitPV"](pbs[k], hh, u.bo, not u.started, st_["last"])
                            u.started = True
                        if st_["last"]:
                            normalize_pair(st_["units"], st_["units"][0].dst)

            def mm(out, lhsT, rhs, start, stop, reads, wkey):
                S.op("pe", L("matmul", out, lhsT, rhs, start=start, stop=stop, skip_group_check=True),
                     reads=reads, writes=[wkey])

            def prompt_stages(kfn, vfn, vkey, sinks, chunk, scale):
                units = tuple(Unit(hh, 512, yT[:, chunk, TS:TS + 512], sinks[hh]) for hh in range(2))
                out = []
                for s_ in range(2):
                    t0 = TS + s_ * 256
                    for m in range(2):
                        def emitS(bs, t0=t0, m=m):
                            mm(banks[bs][:], kfn(t0 + m * 128),
                               qT2[:, :, t0:t0 + 256], True, True, ["kT", "kT2", "qT"], PK(bs))

                        def emitPV(pb, hh, bo, start, stop, s_=s_, m=m):
                            mm(banks[bo][:, s_ * 256:(s_ + 1) * 256], vfn(16 + s_ * 2 + m, hh),
                               PnT[pb][:, hh * 256:(hh + 1) * 256], start, stop, [("PnT", pb), vkey], PK(bo))

                        out.append(dict(units=units, N2=512, scale=scale, emitS=emitS, emitPV=emitPV,
                                        last=(s_ == 1 and m == 1)))
                return out

            def ctx_stages(units, kc_fn, t, v_fn, vkey, scale):
                out = []
                for m in range(2):
                    for qh in range(2):
                        q0 = t * TILE + qh * 256

                        def emitS(bs, m=m, q0=q0):
                            mm(banks[bs][:], kc_fn(m), qT2[:, :, q0:q0 + 256],
                               True, True, ["ckT", "ckbT", "qT"], PK(bs))

                        def emitPV(pb, hh, bo, start, stop, m=m, qh=qh):
                            mm(banks[bo][:, qh * 256:(qh + 1) * 256], v_fn(m, hh), PnT[pb][:, hh * 256:(hh + 1) * 256],
                               start, stop, [("PnT", pb), vkey], PK(bo))

                        out.append(dict(units=units, N2=512, scale=scale, emitS=emitS, emitPV=emitPV, last=False))
                return out

            def na_plan(t):
                plan = []
                for j in range(16):
                    rows = []
                    for r in range(8 * t, 8 * t + 8):
                        rs = min(max(r - 4, 0), 24)
                        lo_in = rs <= 2 * j <= rs + 7
                        hi_in = rs <= 2 * j + 1 <= rs + 7
                        if not (lo_in or hi_in):
                            continue
                        a = 2 * j - r + 7
                        if lo_in and hi_in:
                            cands = [22 - a] + ([10 - a] if 3 <= a <= 9 else [])
                        elif lo_in:
                            assert a == 10
                            cands = [0]
                        else:
                            assert a == 2
                            cands = [8]
                        rows.append((r, cands))
                    if not rows:
                        continue
                    assert [r for r, _ in rows] == list(range(rows[0][0], rows[0][0] + len(rows)))
                    for c0 in range(0, len(rows), 4):
                        part = rows[c0:c0 + 4]
                        runs = []
                        for i, (r, cands) in enumerate(part):
                            if runs and (runs[-1][1] + runs[-1][2]) in cands:
                                runs[-1][2] += 1
                            else:
                                pick = cands[0]
                                if i + 1 < len(part):
                                    for cd in cands:
                                        if cd + 1 in part[i + 1][1]:
                                            pick = cd
                                            break
                                runs.append([i, pick, 1])
                        plan.append((j, part[0][0], len(part), runs))
                return plan

            NA_PLANS = [na_plan(t) for t in range(4)]

            def build_bias(c):
                U = Us[c % 2]
                Gb0 = Gb0s[c % 2]
                ukey = ("U", c % 2)
                gkey = ("Gb0", c % 2)
                for hh in range(2):
                    h = 2 * c + hh
                    for s2 in range(2):
                        S.dma(wq, Gb0[0:64, hh, :, s2, :],
                              bass.AP(scrP_h, (h * 15 + s2) * 127, [[1, 64], [127, 14], [1, 64]]),
                              reads=["scrP"], writes=[gkey])
                for hh in range(2):
                    for piece in range(2):
                        b = nextbank("tr", [6, 7])
                        for a7 in range(7):
                            a_ = 13 - (piece * 7 + a7)
                            S.op("pe", L("matmul", banks[b][:, a7 * 64:(a7 + 1) * 64],
                                         Gb0[0:64, hh, a_, :, :].rearrange("p s k -> p (s k)"), jrev_b[0:64, 0:64],
                                         start=True, stop=True, skip_group_check=True),
                                 reads=[gkey, "jrev_b"], writes=[PK(b)])
                        S.op("dve", L("tensor_tensor", U[:, hh, 9 + piece * 7:16 + piece * 7, :],
                                      banks[b][:, 0:448].rearrange("p (a q) -> p a q", q=64),
                                      negm[:].unsqueeze(1).to_broadcast([128, 7, 64]), ALU.add),
                             reads=[PK(b), "negm"], writes=[ukey])
                    S.op("dve", L("tensor_copy", U[:, hh, 1:8, :], U[:, hh, 13:20, :]), reads=[ukey], writes=[ukey])
                    S.op("dve", L("tensor_copy", U[:, hh, 0, :], U[:, hh, 12, :]), reads=[ukey], writes=[ukey])
                    S.op("dve", L("tensor_copy", U[:, hh, 8, :], U[:, hh, 20, :]), reads=[ukey], writes=[ukey])
                    S.op("dve", L("memset", U[64:128, hh, 0, :], NEG), reads=[ukey], writes=[ukey])
                    S.op("dve", L("memset", U[0:64, hh, 8, :], NEG), reads=[ukey], writes=[ukey])

            def issue_swa_weights(c):
                kv = c // 2
                slot = c % 2
                skey = ("wslot", slot)
                wv = wslot[slot][:, 0:5120].rearrange("p (k n) -> p k n", k=8)
                S.dma(wq, wv[:, :, 0:128], Wd["w_in"][:, 1536 + c * 128:1536 + (c + 1) * 128].rearrange(
                    "(k p) n -> p k n", p=128), writes=[skey])
                for hh in range(2):
                    S.dma(wq, wv[:, :, 256 + hh * 64:256 + (hh + 1) * 64],
                          Wd["w_in"][:, 2048 + kv * 64:2048 + (kv + 1) * 64].rearrange("(k p) n -> p k n", p=128),
                          writes=[skey])
                S.dma(wq, wv[:, :, 512:640], Wd["w_in"][:, 2176:2304].rearrange("(k p) n -> p k n", p=128),
                      writes=[skey])

            def issue_merge_weights(dc):
                slot = dc % 2
                skey = ("wslot", slot)
                wgv = wslot[slot][:, 0:2048].rearrange("p (k n) -> p k n", k=8)
                wbv = wslot[slot][:, 2048:3072].rearrange("p (k n) -> p k n", k=4)
                S.dma(wq, wgv[:, :, 0:128], Wd["w_in"][:, 2304 + dc * 128:2304 + (dc + 1) * 128].rearrange(
                    "(k p) n -> p k n", p=128), writes=[skey])
                S.dma(wq, wgv[:, :, 128:256], Wd["w_in"][:, 3328 + dc * 128:3328 + (dc + 1) * 128].rearrange(
                    "(k p) n -> p k n", p=128), writes=[skey])
                S.dma(wq, wbv[:, :, 0:128], Wd["wb_na"][:, dc * 128:(dc + 1) * 128].rearrange("(k p) n -> p k n", p=128),
                      writes=[skey])
                S.dma(wq, wbv[:, :, 128:256], Wd["wb_sw"][:, dc * 128:(dc + 1) * 128].rearrange("(k p) n -> p k n", p=128),
                      writes=[skey])

            for c in range(4):
                slot = c % 2
                wv = wslot[slot][:, 0:3072].rearrange("p (k n) -> p k n", k=8)
                skey = ("wslot", slot)
                if c > 0:
                    issue_na_weights(c)
                chk(6)
                chk(100 + 10 * c + 0)
                for t in range(NT):
                    b = nextbank("pj", [0, 1, 2, 3])
                    proj_fm(b, wv, 0, t, skey)
                    for hh in range(2):
                        S.op("act", L("activation", qT2[hh * 64:hh * 64 + 64, hh, t * TILE:(t + 1) * TILE],
                                      banks[b][hh * 64:hh * 64 + 64, :], AF.Copy, scale=0.125),
                             reads=[PK(b)], writes=["qT"])
                    b = nextbank("pj", [0, 1, 2, 3])
                    proj_fm(b, wv, 128, t, skey)
                    S.op("act", L("copy", kT[:, t * TILE:(t + 1) * TILE], banks[b][:]),
                         reads=[PK(b)], writes=["kT"])
                for blk in range(20):
                    b = nextbank("pj", [0, 1, 2, 3])
                    proj_tm(b, wv, 256, 128, blk * 128, skey)
                    S.op("act", L("copy", vA[:, blk, 64:192], banks[b][:, 0:128]),
                        reads=[PK(b)], writes=["vA"])
                    if blk >= 16:
                        ks = nextbank("kvst", [0, 1])
                        S.op("act", L("copy", kvst[ks][:], banks[b][:, 0:128]), reads=[PK(b)],
                             writes=[("kvst", ks)])
                        S.dma("sp", nv_na[(blk - 16) * 128:(blk - 15) * 128, c * 128:(c + 1) * 128], kvst[ks][:],
                              reads=[("kvst", ks)], final=True)
                        b2 = nextbank("pj", [0, 1, 2, 3])
                        proj_tm(b2, wv, 128, 128, blk * 128, skey)
                        ks = nextbank("kvst", [0, 1])
                        S.op("act", L("copy", kvst[ks][:], banks[b2][:, 0:128]), reads=[PK(b2)],
                             writes=[("kvst", ks)])
                        S.dma("sp", nk_na[(blk - 16) * 128:(blk - 15) * 128, c * 128:(c + 1) * 128], kvst[ks][:],
                              reads=[("kvst", ks)], final=True)
                chk(7)
                chk(100 + 10 * c + 1)
                if c == 0:
                    prep_caches()
                    build_bias(0)
                if c + 1 < 4:
                    build_bias(c + 1)
                U = Us[c % 2]
                ukey = ("U", c % 2)
                stages = []
                for t in range(4):
                    units = tuple(Unit(hh, 512, yT[:, c, t * TILE:(t + 1) * TILE], None)
                                  for hh in range(2))
                    stages += ctx_stages(units, lambda m: ckT[:, c, m * 128:(m + 1) * 128], t,
                                         lambda m, hh: cvA[:, m, c, hh * 128:hh * 128 + 128], "cvA", 1.0)
                    plan = NA_PLANS[t]
                    for pi, (j, r0, nr, runs) in enumerate(plan):
                        N = nr * 64
                        q0 = (r0 - 8 * t) * 64

                        def emitS(bs, j=j, r0=r0, N=N, runs=runs, U=U, ukey=ukey):
                            if len(runs) == 1:
                                off, idx0, n = runs[0]
                                mm(banks[bs][:, 0:2 * N], ident_b[:],
                                   U[:, :, idx0:idx0 + n, :].rearrange("p h a q -> p h (a q)"),
                                   True, False, [ukey, "ident_b"], PK(bs))
                            else:
                                first = True
                                for (off, idx0, n) in runs:
                                    for hh in range(2):
                                        mm(banks[bs][:, hh * N + off * 64:hh * N + (off + n) * 64], ident_b[:],
                                           U[:, hh, idx0:idx0 + n, :].rearrange("p a q -> p (a q)"),
                                           first, False, [ukey, "ident_b"], PK(bs))
                                        first = False
                            mm(banks[bs][:, 0:2 * N], kT[:, j * 128:(j + 1) * 128], qT2[:, :, r0 * 64:r0 * 64 + N],
                               False, True, ["kT", "qT"], PK(bs))

                        def emitPV(pb, hh, bo, start, stop, j=j, N=N, q0=q0):
                            mm(banks[bo][:, q0:q0 + N], vA[:, j, hh * 128:hh * 128 + 128], PnT[pb][:, hh * N:(hh + 1) * N],
                               start, stop, [("PnT", pb), "vA"], PK(bo))

                        stages.append(dict(units=units, N2=2 * N, scale=1.0, emitS=emitS, emitPV=emitPV,
                                           last=(pi == len(plan) - 1)))
                chk(8)
                chk(100 + 10 * c + 2)
                stages += prompt_stages(lambda tok: kT[:, tok:tok + 128],
                                        lambda blk, hh: vA[:, blk, hh * 128:hh * 128 + 128], "vA", (None, None), c, 1.0)
                run_pipeline(stages)
                chk(12)
                chk(100 + 10 * c + 3)

            chk(9)
            issue_swa_weights(0)
            S.barrier()
            close_scope(nas)
            sws = new_scope()
            kT = None
            kT2 = sb("kT2", [128, 2, T], BF16, sws)
            vB = sb("vB", [128, 20, 256], BF16, sws)
            bandR = sb("bandR", [128, 3, 128], BF16, sws)
            cosT = sb("cosT", [128, TS], F32, sws)
            sinT = sb("sinT", [128, TS], F32, sws)
            rt = [sb(f"rt{i}", [128, 512], F32, sws) for i in range(2)]
            S.dma(wq, bandR[:].rearrange("p s n -> p (s n)"), c_band, writes=["bandR"])
            S.dma("sp", cosT[:], c_cos, writes=["cosT"])
            S.dma("sp", sinT[:], c_sin, writes=["sinT"])
            S.op("pool", L("memset", vB[:], 1.0), writes=["vB"])
            for c in range(4):
                kv = c // 2
                slot = c % 2
                skey = ("wslot", slot)
                wv = wslot[slot][:, 0:5120].rearrange("p (k n) -> p k n", k=8)
                if c > 0:
                    issue_swa_weights(c)
                for base in (0, 256):
                    src = wv[:, :, base:base + 128].rearrange("p k (g j i) -> p k g j i", g=4, j=2)
                    dst = wv[:, :, base + 128:base + 256].rearrange("p k (g j i) -> p k g j i", g=4, j=2)
                    for j in range(2):
                        S.op("pool", L("tensor_copy", dst[:, :, :, j, :], src[:, :, :, 1 - j, :]),
                             reads=[skey], writes=[skey])

                def rope_proj(col0, t, dsts, dkey):
                    b1 = nextbank("pj", [0, 1, 2, 3])
                    proj_fm(b1, wv, col0, t, skey)
                    if t == 4:
                        for psl, dap in dsts:
                            S.op("dve", L("tensor_copy", dap, banks[b1][psl, :]), reads=[PK(b1)], writes=[dkey])
                        return
                    b2 = nextbank("pj", [0, 1, 2, 3])
                    proj_fm(b2, wv, col0 + 128, t, skey)
                    sl = slice(t * TILE, (t + 1) * TILE)
                    r1 = nextbank("rt", [0, 1])
                    S.op("dve", L("tensor_tensor", rt[r1][:], banks[b1][:], cosT[:, sl], ALU.mult),
                         reads=[PK(b1), "cosT"], writes=[("rt", r1)])
                    r2 = nextbank("rt", [0, 1])
                    S.op("dve", L("tensor_tensor", rt[r2][:], banks[b2][:], sinT[:, sl], ALU.mult),
                         reads=[PK(b2), "sinT"], writes=[("rt", r2)])
                    for psl, dap in dsts:
                        S.op("dve", L("tensor_tensor", dap, rt[r1][psl, :], rt[r2][psl, :], ALU.add),
                             reads=[("rt", r1), ("rt", r2)], writes=[dkey])

                if c % 2 == 0:
                    for blk in range(20):
                        b = nextbank("pj", [0, 1, 2, 3])
                        proj_tm(b, wv, 512, 128, blk * 128, skey)
                        for r_ in range(2):
                            S.op("act", L("copy", vB[:, blk, 64 + r_ * 64:128 + r_ * 64], banks[b][:, kv * 64:(kv + 1) * 64]),
                                 reads=[PK(b)], writes=["vB"])
                        if c == 0 and blk >= 16:
                            ks = nextbank("kvst", [0, 1])
                            S.op("act", L("copy", kvst[ks][:], banks[b][:, 0:128]), reads=[PK(b)],
                                 writes=[("kvst", ks)])
                            S.dma("sp", nv_sw[(blk - 16) * 128:(blk - 15) * 128, :], kvst[ks][:],
                                  reads=[("kvst", ks)], final=True)
                        if blk >= 16:
                            b2 = nextbank("pj", [0, 1, 2, 3])
                            proj_tm(b2, wv, 256, 64, blk * 128, skey)
                            ks = nextbank("kvst", [0, 1])
                            S.op("act", L("copy", kvst[ks][:, 0:64], banks[b2][:, 0:64]),
                                 reads=[PK(b2)], writes=[("kvst", ks)])
                            S.dma("sp", nk_sw[(blk - 16) * 128:(blk - 15) * 128, kv * 64:(kv + 1) * 64],
                                  kvst[ks][:, 0:64], reads=[("kvst", ks)], final=True)
                for t in range(NT):
                    rope_proj(0, t, [(slice(hh * 64, hh * 64 + 64), qT2[hh * 64:hh * 64 + 64, hh, t * TILE:(t + 1) * TILE])
                                     for hh in range(2)], "qT")
                    if c % 2 == 0:
                        rope_proj(256, t, [(slice(0, 128), kT2[:, kv, t * TILE:(t + 1) * TILE])], "kT2")
                stages = []
                for t in range(4):
                    units = tuple(Unit(hh, 512, yT[:, 4 + c, t * TILE:(t + 1) * TILE], 2 * c + hh)
                                  for hh in range(2))
                    stages += ctx_stages(units, lambda m: ckbT[:, kv, m * 128:(m + 1) * 128], t,
                                         lambda m, hh: cvB[:, m, kv, hh * 128:hh * 128 + 128], "cvB", 0.125)
                    mlist = list(range(max(0, 4 * t - 1), min(15, 4 * t + 4) + 1))
                    parts = []
                    for mb in mlist:
                        nlo = max(4 * t, mb - 1)
                        nhi = min(4 * t + 3, mb + 1)
                        for n0 in range(nlo, nhi + 1, 2):
                            parts.append((mb, n0, min(nhi, n0 + 1)))
                    for pi, (mb, nlo, nhi) in enumerate(parts):
                        N = (nhi - nlo + 1) * 128
                        q0 = (nlo - 4 * t) * 128

                        def emitS(bs, mb=mb, nlo=nlo, nhi=nhi, N=N):
                            first = True
                            for hh in range(2):
                                for n_ in range(nlo, nhi + 1):
                                    if n_ == mb:
                                        continue
                                    c0 = hh * N + (n_ - nlo) * 128
                                    mm(banks[bs][:, c0:c0 + 128], ident_b[:], bandR[:, n_ - mb + 1, :], first, False,
                                       ["bandR", "ident_b"], PK(bs))
                                    first = False
                            mm(banks[bs][:, 0:2 * N], kT2[:, kv, mb * 128:(mb + 1) * 128], qT2[:, :, nlo * 128:(nhi + 1) * 128],
                               first, True, ["kT2", "qT"], PK(bs))

                        def emitPV(pb, hh, bo, start, stop, mb=mb, N=N, q0=q0):
                            mm(banks[bo][:, q0:q0 + N], vB[:, mb, hh * 128:hh * 128 + 128], PnT[pb][:, hh * N:(hh + 1) * N],
                               start, stop, [("PnT", pb), "vB"], PK(bo))

                        stages.append(dict(units=units, N2=2 * N, scale=0.125, emitS=emitS, emitPV=emitPV,
                                           last=(pi == len(parts) - 1)))
                stages += prompt_stages(lambda tok, kv=kv: kT2[:, kv, tok:tok + 128],
                                        lambda blk, hh: vB[:, blk, hh * 128:hh * 128 + 128], "vB",
                                        (2 * c, 2 * c + 1), 4 + c, 0.125)
                run_pipeline(stages)

            chk(10)
            issue_merge_weights(0)
            S.barrier()
            close_scope(sws)
            close_scope(mx)
            mg = new_scope()
            mergedT = sb("mergedT", [128, 8, T], BF16, mg)
            sg1 = [sb(f"sg1_{i}", [128, 512], F32, mg) for i in range(2)]
            sg2 = [sb(f"sg2_{i}", [128, 512], F32, mg) for i in range(2)]
            xnt = [sb(f"xnt{i}", [128, NCH, 512], F32, mg) for i in range(2)]
            for dc in range(NCH):
                slot = dc % 2
                skey = ("wslot", slot)
                wgv = wslot[slot][:, 0:2048].rearrange("p (k n) -> p k n", k=8)
                wbv = wslot[slot][:, 2048:3072].rearrange("p (k n) -> p k n", k=4)
                if dc > 0:
                    issue_merge_weights(dc)
                for t in range(NT):
                    sl = slice(t * TILE, (t + 1) * TILE)
                    bg1 = nextbank("mg", [0, 1, 2, 3]); bg2 = nextbank("mg", [0, 1, 2, 3])
                    bb1 = nextbank("mg", [0, 1, 2, 3]); bb2 = nextbank("mg", [0, 1, 2, 3])
                    proj_fm(bg1, wgv, 0, t, skey)
                    proj_fm(bg2, wgv, 128, t, skey)
                    for (bb, off, ch0) in ((bb1, 0, 0), (bb2, 128, 4)):
                        for k in range(4):
                            S.op("pe", L("matmul", banks[bb][:], wbv[:, k, off:off + 128], yT[:, ch0 + k, sl], start=(k == 0), stop=(k == 3)),
                                reads=[skey, "yT"], writes=[PK(bb)])
                    i1 = nextbank("sg1", [0, 1]); i2 = nextbank("sg2", [0, 1])
                    S.op("act", L("activation", sg1[i1][:], banks[bg1][:], AF.Sigmoid),
                         reads=[PK(bg1)], writes=[("sg1", i1)])
                    S.op("act", L("activation", sg2[i2][:], banks[bg2][:], AF.Sigmoid),
                         reads=[PK(bg2)], writes=[("sg2", i2)])
                    S.op("dve", L("tensor_tensor", sg1[i1][:], banks[bb1][:], sg1[i1][:], ALU.mult),
                         reads=[PK(bb1), ("sg1", i1)], writes=[("sg1", i1)])
                    S.op("dve", L("tensor_tensor", sg2[i2][:], banks[bb2][:], sg2[i2][:], ALU.mult),
                         reads=[PK(bb2), ("sg2", i2)], writes=[("sg2", i2)])
                    S.op("dve", L("tensor_tensor", mergedT[:, dc, sl], sg1[i1][:], sg2[i2][:], ALU.add),
                        reads=[("sg1", i1), ("sg2", i2)], writes=[("mergedT", t)])
            chk(11)
            wov = []
            yflat = yT[:].rearrange("p c t -> p (c t)")
            wflat = xnt[1][:].rearrange("p c n -> p (c n)").bitcast(BF16)
            for half in range(2):
                wv_ = wflat[:, half * 4096:(half + 1) * 4096].rearrange("p (k n) -> p k n", k=8)
                S.dma(wq, wv_, Wd["w_out"][:, half * 512:(half + 1) * 512].rearrange("(k p) n -> p k n", p=128),
                      writes=[("xnt", 1)])
                wov.append(wv_)
            ffn_issue(0, Wd["f2g"], Wd["f2u"], Wd["f2d"])
            ffn_issue(1, Wd["f2g"], Wd["f2u"], Wd["f2d"])
            xnt = [xnt[0],
                   yflat[:, 8192:16384].bitcast(F32).rearrange("p (c n) -> p c n", c=8),
                   yflat[:, 0:8192].bitcast(F32).rearrange("p (c n) -> p c n", c=8)]
            xnt_keys = [("xnt", 0), ("xnt", 2), ("xnt", 3)]
            for t in range(NT):
                kind = 0 if t < 4 else 1
                sl = slice(t * TILE, (t + 1) * TILE)
                xb = nextbank("xnt", list(range(len(xnt))))
                xkey = xnt_keys[xb]
                S.dma("sp", xnt[xb][:], scrX[:, :, sl], reads=[("scrX", t)] + (["yT"] if xb > 0 else []),
                      writes=[xkey] + (["yT"] if xb > 0 else []))
                for dc in range(NCH):
                    half, dci = divmod(dc, 4)
                    b = nextbank("wo", [0, 1, 2, 3, 4, 5])
                    for k in range(8):
                        S.op("pe", L("matmul", banks[b][:], wov[half][:, k, dci * 128:(dci + 1) * 128], mergedT[:, k, sl],
                            start=(k == 0), stop=(k == 7)),
                            reads=[("xnt", 1), ("mergedT", t)], writes=[PK(b)])
                    S.op("dve", L("scalar_tensor_tensor", xnt[xb][:, dc, :], banks[b][:], scal[:, 1, 2, dc, kind:kind + 1],
                                  xnt[xb][:, dc, :], ALU.mult, ALU.add),
                        reads=[PK(b), xkey, ("scal", 1, 2)], writes=[xkey])
                S.dma("sp", scrX[:, :, sl], xnt[xb][:], reads=[xkey], writes=[("scrX", t)])
                norm_mod_tile(lambda c, xb=xb: xnt[xb][:, c, :], xkey, 2, t)
            S.barrier()
            close_scope(mg)
            close_scope(ys)

            f2 = new_scope()
            xT = sb("xT2", [128, NCH, T], F32, f2)
            actb = [sb(f"actb2_{i}", [128, 2, 512], BF16, f2) for i in range(3)]
            sgb = [sb(f"sgb2_{i}", [128, 512], F32, f2) for i in range(2)]
            stg = [sb(f"stg2_{i}", [128, D], F32, f2) for i in range(4)]
            for t in range(NT):
                S.dma("sp", xT[:, :, t * TILE:(t + 1) * TILE], scrX[:, :, t * TILE:(t + 1) * TILE],
                      reads=[("scrX", t)], writes=[("xT", t)])
            if stop == 4:
                return stop_here(xT, f2)
            def hook2A(t):
                return norm_stats(xtile(xT, t), ("xT", t))

            def hook2B(t, r):
                norm_recip(r)
                sl = slice(t * TILE, (t + 1) * TILE)
                for c in range(NCH):
                    S.op("dve", L("scalar_tensor_tensor", xT[:, c, sl], xT[:, c, sl], gfin[:, c:c + 1], rstd[r][:], ALU.mult, ALU.mult),
                        reads=[("xT", t), ("rstd", r), "prm"], writes=[("xT", t)])

            def hook2C(t):
                for tb in range(4):
                    tok0 = t * TILE + tb * 128
                    so = nextbank("stg", [0, 1, 2, 3])
                    for half in range(2):
                        b = nextbank("trf", [0, 1, 2, 3, 4, 5])
                        for j in range(4):
                            c = half * 4 + j
                            S.op("pe", L("transpose", banks[b][:, j * 128:(j + 1) * 128], xT[:, c, tok0:tok0 + 128], ident_f[:]),
                                reads=[("xT", t), "ident_f"], writes=[PK(b)])
                        S.op("act", L("copy", stg[so][:, half * 512:(half + 1) * 512], banks[b][:]), reads=[PK(b)],
                             writes=[("stg", so)])
                    S.dma("sp", y_tok[tok0:tok0 + 128, :], stg[so][:], reads=[("stg", so)], writes=[("yout", tok0)],
                          final=True)

            ffn(xT, 2, Wd["f2g"], Wd["f2u"], Wd["f2d"], actb, sgb, None, None, None, (4, 5, 6, 7), 2)
            rr = {0: hook2A(0), 1: hook2A(1)}
            hook2B(0, rr[0])
            for t in range(NT):
                if t + 2 < NT:
                    rr[t + 2] = hook2A(t + 2)
                if t + 1 < NT:
                    hook2B(t + 1, rr[t + 1])
                hook2C(t)
            S.emit()
            close_scope(f2)

        except _Stop:
            S.emit()
            while open_scopes:
                close_scope(open_scopes[-1])
    return nc


def _consts():
    ident = np.eye(128, dtype=np.float32)
    jrev = np.zeros((128, 128), np.float32)
    for s in range(2):
        for k in range(64):
            jrev[s * 64 + k, s * 64 + 63 - k] = 1.0
    tok = np.arange(TS)
    row = (tok // 64).astype(np.float32)
    col = (tok % 64).astype(np.float32)
    freqs = np.power(np.float32(10000.0), -np.arange(16, dtype=np.float32) / np.float32(16)).astype(np.float32)
    cos = np.zeros((128, TS), np.float32)
    sin = np.zeros((128, TS), np.float32)
    for p in range(128):
        d = p % 64
        pos = row if d < 32 else col
        dd = d % 32
        f = freqs[dd % 16]
        ang = (pos * f).astype(np.float32)
        cos[p] = np.cos(ang)
        sin[p] = np.sin(ang) * (-1.0 if dd < 16 else 1.0)
    kj = np.arange(128)[:, None]
    qi = np.arange(128)[None, :]
    band = np.zeros((128, 3, 128), np.float32)
    band[:, 0, :] = np.where(kj <= qi, 0.0, NEG)
    band[:, 2, :] = np.where(kj >= qi, 0.0, NEG)
    qc = np.arange(64)
    cstart = np.clip(qc - 8, 0, 48)
    kc = np.arange(64)[:, None]
    ok = (kc >= cstart[None, :]) & (kc < cstart[None, :] + 16)
    negm = np.where(ok, 0.0, NEG).astype(np.float32)
    negm = np.concatenate([negm, negm], axis=0)
    return dict(c_ident=ident, c_jrev=jrev, c_cos=cos, c_sin=sin,
                c_band=np.ascontiguousarray(band.reshape(128, 384)), c_negm=np.ascontiguousarray(negm))


_NC_CACHE = {}


def kernel(**inputs):
    return _run(_prep(**inputs))


def _prep(x_prompt, x_sample, cache_na_k, cache_na_v, cache_swa_k, cache_swa_v, c, c_ctx,
          w_ada, b_ada, norm_ffn1, ffn1_w_gate, ffn1_w_up, ffn1_w_down, norm_mix, w_in,
          na_rel_bias, swa_sink, w_branch_na, w_branch_swa, w_out, norm_ffn2,
          ffn2_w_gate, ffn2_w_up, ffn2_w_down, norm_final):
    f = lambda a: np.ascontiguousarray(np.asarray(a, dtype=np.float32))
    x_prompt, x_sample = f(x_prompt), f(x_sample)
    shared = {
        "w_ada": f(w_ada)[0], "b_ada": f(b_ada)[0], "norm_ffn1": f(norm_ffn1)[0],
        "f1g": f(ffn1_w_gate)[0], "f1u": f(ffn1_w_up)[0], "f1d": f(ffn1_w_down)[0],
        "norm_mix": f(norm_mix)[0], "w_in": f(w_in)[0],
        "rel_bias": f(na_rel_bias)[0].reshape(120, 31), "sink": f(swa_sink)[0],
        "wb_na": f(w_branch_na)[0], "wb_sw": f(w_branch_swa)[0], "w_out": f(w_out)[0],
        "norm_ffn2": f(norm_ffn2)[0], "f2g": f(ffn2_w_gate)[0], "f2u": f(ffn2_w_up)[0], "f2d": f(ffn2_w_down)[0],
        "norm_final": f(norm_final),
    }
    shared.update(_consts())
    cc, cx = f(c), f(c_ctx)
    cnk, cnv, csk, csv = f(cache_na_k), f(cache_na_v), f(cache_swa_k), f(cache_swa_v)
    in_maps = []
    for i in range(8):
        m = dict(shared)
        m["x_tok"] = np.ascontiguousarray(np.concatenate(
            [x_sample[i], x_prompt[2 * i], x_prompt[2 * i + 1]], axis=0))
        m["cond"] = np.ascontiguousarray(np.stack([cc[i], cx], axis=0))
        m["cna_k"] = np.ascontiguousarray(cnk[i, 0].reshape(256, 512))
        m["cna_v"] = np.ascontiguousarray(cnv[i, 0].reshape(256, 512))
        m["csw_k"] = np.ascontiguousarray(csk[i, 0].reshape(256, 128))
        m["csw_v"] = np.ascontiguousarray(csv[i, 0].reshape(256, 128))
        in_maps.append(m)
    return in_maps


def _run(in_maps):
    if "nc" not in _NC_CACHE:
        _NC_CACHE["nc"] = build()
    res = run_bass_kernel_spmd(_NC_CACHE["nc"], in_maps, core_ids=list(range(8)))
    R = res.results
    y_sample = np.stack([R[i]["y_tok"][:TS] for i in range(8)], axis=0)
    y_prompt = np.stack([R[i // 2]["y_tok"][TS + (i % 2) * 256:TS + (i % 2 + 1) * 256] for i in range(16)], axis=0)

    def gather(name, nh):
        out = np.stack([R[i // 2][name][(i % 2) * 256:(i % 2 + 1) * 256] for i in range(16)], axis=0)
        return np.ascontiguousarray(out.reshape(16, 1, 256, nh, 64).astype(np.float32))

    return (np.ascontiguousarray(y_prompt.astype(np.float32)), np.ascontiguousarray(y_sample.astype(np.float32)),
            gather("nk_na", 8), gather("nv_na", 8), gather("nk_sw", 2), gather("nv_sw", 2))
```
